# Optimizing a Trainium2 kernel written in Bass

```python
import jax, jax.numpy as jnp
from jax import lax
import numpy as np

D_MODEL = 1024
BATCH = 8
SEQ = 8192
DEPTH = 1

GRID_W = 64
N_Q_HEADS = 16
N_KV_HEADS = 4
HEAD_DIM = 64
ROPE_THETA = 10000.0
Q_BLOCK = 128
SSD_EXPAND = 2
D_INNER = SSD_EXPAND * D_MODEL
SSD_HEAD_DIM = 64
N_SSD_HEADS = D_INNER // SSD_HEAD_DIM
N_SSD_GROUPS = 4
D_STATE = 128
D_CONV = 5
CHUNK = 128
D_FF = 4 * D_MODEL
EPS = 1e-6

ATTN_Q_DIM = N_Q_HEADS * HEAD_DIM
ATTN_KV_DIM = N_KV_HEADS * HEAD_DIM
CONV_DIM = D_INNER + 2 * N_SSD_GROUPS * D_STATE
D_IN_PROJ = ATTN_Q_DIM + 2 * ATTN_KV_DIM + CONV_DIM + D_INNER + 2 * N_SSD_HEADS + 2 * D_MODEL

kernel_name = "hybrid_gqa_ssd_griffin_merge_block"


def rms_norm(x, w):
    xf = x.astype(jnp.float32)
    xf = xf * lax.rsqrt(jnp.mean(xf * xf, axis=-1, keepdims=True) + EPS)
    return xf.astype(x.dtype) * w


def rotate(u, cos, sin):
    f = u.shape[-1] // 2
    u1, u2 = u[..., :f], u[..., f:]
    cos = cos[None, :, None, :]
    sin = sin[None, :, None, :]
    return jnp.concatenate([u1 * cos - u2 * sin, u2 * cos + u1 * sin], axis=-1)


def axial_rope(t, cos_r, sin_r, cos_c, sin_c):
    half = t.shape[-1] // 2
    out = jnp.concatenate([rotate(t[..., :half], cos_r, sin_r),
                           rotate(t[..., half:], cos_c, sin_c)], axis=-1)
    return out.astype(t.dtype)


def block_attention(q, k, v):
    b, s, hq, dh = q.shape
    hkv = k.shape[2]
    rep = hq // hkv
    nb = s // Q_BLOCK
    qb = q.reshape(b, nb, Q_BLOCK, hkv, rep, dh).transpose(1, 0, 2, 3, 4, 5)
    scale = dh ** -0.5

    def one_block(qi):
        sc = jnp.einsum("bqgrd,bkgd->bgrqk", qi, k).astype(jnp.float32) * scale
        p = jax.nn.softmax(sc, axis=-1).astype(v.dtype)
        return jnp.einsum("bgrqk,bkgd->bqgrd", p, v)

    out = lax.map(one_block, qb)
    return out.transpose(1, 0, 2, 3, 4, 5).reshape(b, s, hq * dh)


def segsum(a):
    t = a.shape[-1]
    cs = jnp.cumsum(a, axis=-1)
    diff = cs[..., :, None] - cs[..., None, :]
    mask = jnp.tril(jnp.ones((t, t), dtype=bool))
    return jnp.where(mask, diff, -jnp.inf)


def ssd_chunked(xdt, a, bm, cm):
    b, s, h, p = xdt.shape
    g, n = bm.shape[2], bm.shape[3]
    r = h // g
    nc = s // CHUNK
    X = xdt.astype(jnp.float32).reshape(b, nc, CHUNK, g, r, p)
    A = a.reshape(b, nc, CHUNK, g, r).transpose(0, 3, 4, 1, 2)
    Bc = bm.astype(jnp.float32).reshape(b, nc, CHUNK, g, n)
    Cc = cm.astype(jnp.float32).reshape(b, nc, CHUNK, g, n)
    A_cs = jnp.cumsum(A, axis=-1)
    CB = jnp.einsum("bclgn,bcsgn->bgcls", Cc, Bc)
    M = CB[:, :, None] * jnp.exp(segsum(A))
    y_diag = jnp.einsum("bgrcls,bcsgrp->bclgrp", M, X)
    decay_states = jnp.exp(A_cs[..., -1:] - A_cs)
    states = jnp.einsum("bclgn,bgrcl,bclgrp->cbgrpn", Bc, decay_states, X)
    chunk_decay = jnp.moveaxis(jnp.exp(A_cs[..., -1]), -1, 0)

    def step(hs, inp):
        st, dec = inp
        return dec[..., None, None] * hs + st, hs

    h0 = jnp.zeros(states.shape[1:], jnp.float32)
    _, prev = lax.scan(step, h0, (states, chunk_decay))
    y_off = jnp.einsum("bclgn,cbgrpn,bgrcl->bclgrp", Cc, prev, jnp.exp(A_cs))
    return (y_diag + y_off).reshape(b, s, h, p)


def depthwise_conv_centred(u, w, bias):
    pad = (w.shape[0] - 1) // 2
    out = lax.conv_general_dilated(u, w[:, None, :].astype(u.dtype), window_strides=(1,),
                                   padding=[(pad, pad)], dimension_numbers=("NWC", "WIO", "NWC"),
                                   feature_group_count=u.shape[-1])
    return out + bias


def ssd_mixer(xBC, z, dt_raw, conv_w, conv_b, A_log, dt_bias, ssd_D, ssd_norm_w):
    b, s, _ = xBC.shape
    xBC = jax.nn.silu(depthwise_conv_centred(xBC, conv_w, conv_b))
    gn = N_SSD_GROUPS * D_STATE
    xs = xBC[..., :D_INNER].reshape(b, s, N_SSD_HEADS, SSD_HEAD_DIM)
    bm = xBC[..., D_INNER:D_INNER + gn].reshape(b, s, N_SSD_GROUPS, D_STATE)
    cm = xBC[..., D_INNER + gn:].reshape(b, s, N_SSD_GROUPS, D_STATE)
    dt = jax.nn.softplus(dt_raw.astype(jnp.float32).reshape(b, s, 2, N_SSD_HEADS)
                         + dt_bias.astype(jnp.float32))
    A = -jnp.exp(A_log.astype(jnp.float32))
    dt_f, dt_b = dt[:, :, 0], dt[:, :, 1]
    xf = xs.astype(jnp.float32)
    y_fwd = ssd_chunked(xf * dt_f[..., None], dt_f * A[0], bm, cm)
    flip = lambda t: jnp.flip(t, axis=1)
    y_bwd = flip(ssd_chunked(flip(xf * dt_b[..., None]), flip(dt_b * A[1]), flip(bm), flip(cm)))
    y = y_fwd + y_bwd + ssd_D.astype(jnp.float32)[:, None] * xf
    y = y.reshape(b, s, D_INNER).astype(xBC.dtype)
    return rms_norm(y * jax.nn.silu(z), ssd_norm_w)


def setup_inputs(seed: int = 0) -> dict:
    key = jax.random.key(seed)
    ks = jax.random.split(key, 20)
    f32 = jnp.float32
    L = DEPTH
    nrm = lambda k, shape, s: jax.random.normal(k, shape, f32) * s
    x = jax.random.normal(ks[0], (BATCH, SEQ, D_MODEL), f32)
    c = jax.random.normal(ks[1], (BATCH, D_MODEL), f32)
    w_ada = nrm(ks[2], (L, D_MODEL, 6 * D_MODEL), 0.5 * D_MODEL ** -0.5)
    b_ada = nrm(ks[3], (L, 6 * D_MODEL), 0.02)
    norm1_w = 1.0 + nrm(ks[4], (L, D_MODEL), 0.02)
    norm2_w = 1.0 + nrm(ks[5], (L, D_MODEL), 0.02)
    w_in = nrm(ks[6], (L, D_MODEL, D_IN_PROJ), D_MODEL ** -0.5)
    q_norm_w = 1.0 + nrm(ks[7], (L, HEAD_DIM), 0.02)
    k_norm_w = 1.0 + nrm(ks[8], (L, HEAD_DIM), 0.02)
    conv_w = nrm(ks[9], (L, D_CONV, CONV_DIM), D_CONV ** -0.5)
    conv_b = nrm(ks[10], (L, CONV_DIM), 0.02)
    A_log = jnp.log(jax.random.uniform(ks[11], (L, 2, N_SSD_HEADS), f32, 1.0, 16.0))
    dt0 = jnp.exp(jax.random.uniform(ks[12], (L, 2, N_SSD_HEADS), f32, np.log(1e-3), np.log(1e-1)))
    dt_bias = dt0 + jnp.log(-jnp.expm1(-dt0))
    ssd_D = 1.0 + nrm(ks[13], (L, N_SSD_HEADS), 0.02)
    ssd_norm_w = 1.0 + nrm(ks[14], (L, D_INNER), 0.02)
    w_attn_out = nrm(ks[15], (L, ATTN_Q_DIM, D_MODEL), ATTN_Q_DIM ** -0.5)
    w_ssd_out = nrm(ks[16], (L, D_INNER, D_MODEL), D_INNER ** -0.5)
    w_o = nrm(ks[17], (L, D_MODEL, D_MODEL), D_MODEL ** -0.5)
    w_mlp1 = nrm(ks[18], (L, D_MODEL, D_FF), D_MODEL ** -0.5)
    w_mlp2 = nrm(ks[19], (L, D_FF, D_MODEL), D_FF ** -0.5)
    return {"x": x, "c": c, "w_ada": w_ada, "b_ada": b_ada, "norm1_w": norm1_w, "norm2_w": norm2_w,
            "w_in": w_in, "q_norm_w": q_norm_w, "k_norm_w": k_norm_w, "conv_w": conv_w, "conv_b": conv_b,
            "A_log": A_log, "dt_bias": dt_bias, "ssd_D": ssd_D, "ssd_norm_w": ssd_norm_w,
            "w_attn_out": w_attn_out, "w_ssd_out": w_ssd_out, "w_o": w_o,
            "w_mlp1": w_mlp1, "w_mlp2": w_mlp2}


def reference(x, c, w_ada, b_ada, norm1_w, norm2_w, w_in, q_norm_w, k_norm_w, conv_w, conv_b,
              A_log, dt_bias, ssd_D, ssd_norm_w, w_attn_out, w_ssd_out, w_o, w_mlp1, w_mlp2):
    b, s, d = x.shape
    rows = s // GRID_W
    pos_row = jnp.repeat(jnp.arange(rows, dtype=jnp.int32), GRID_W).astype(jnp.float32)
    pos_col = jnp.tile(jnp.arange(GRID_W, dtype=jnp.int32), rows).astype(jnp.float32)
    axis_dim = HEAD_DIM // 2
    inv_freq = ROPE_THETA ** (-jnp.arange(0, axis_dim, 2, dtype=jnp.float32) / axis_dim)
    ang_r = pos_row[:, None] * inv_freq[None, :]
    ang_c = pos_col[:, None] * inv_freq[None, :]
    cos_r, sin_r = jnp.cos(ang_r), jnp.sin(ang_r)
    cos_c, sin_c = jnp.cos(ang_c), jnp.sin(ang_c)

    sizes = [ATTN_Q_DIM, ATTN_KV_DIM, ATTN_KV_DIM, CONV_DIM, D_INNER, 2 * N_SSD_HEADS, 2 * D_MODEL]
    offsets = []
    acc = 0
    for sz in sizes[:-1]:
        acc += sz
        offsets.append(acc)

    for l in range(DEPTH):
        mod = jax.nn.silu(c) @ w_ada[l] + b_ada[l]
        shift1, scale1, gate1, shift2, scale2, gate2 = [m[:, None, :] for m in jnp.split(mod, 6, axis=-1)]

        h = rms_norm(x, norm1_w[l]) * (1.0 + scale1) + shift1
        proj = h @ w_in[l]
        q, k, v, xBC, z, dt_raw, gates = jnp.split(proj, offsets, axis=-1)

        q = rms_norm(q.reshape(b, s, N_Q_HEADS, HEAD_DIM), q_norm_w[l])
        k = rms_norm(k.reshape(b, s, N_KV_HEADS, HEAD_DIM), k_norm_w[l])
        v = v.reshape(b, s, N_KV_HEADS, HEAD_DIM)
        q = axial_rope(q, cos_r, sin_r, cos_c, sin_c)
        k = axial_rope(k, cos_r, sin_r, cos_c, sin_c)
        attn = block_attention(q, k, v)

        ssd = ssd_mixer(xBC, z, dt_raw, conv_w[l], conv_b[l], A_log[l], dt_bias[l], ssd_D[l], ssd_norm_w[l])

        g_attn = jax.nn.sigmoid(gates[..., :D_MODEL])
        g_ssd = jax.nn.sigmoid(gates[..., D_MODEL:])
        merged = g_attn * (attn @ w_attn_out[l]) + g_ssd * (ssd @ w_ssd_out[l])
        x = x + gate1 * (merged @ w_o[l])

        h2 = rms_norm(x, norm2_w[l]) * (1.0 + scale2) + shift2
        ff = jnp.square(jax.nn.relu(h2 @ w_mlp1[l])) @ w_mlp2[l]
        x = x + gate2 * ff
    return x
```

```python
import numpy as np
import ml_dtypes
from contextlib import ExitStack
import concourse.bass as bass
import concourse.mybir as mybir
from concourse.bass_utils import run_bass_kernel_spmd

F32, BF16 = mybir.dt.float32, mybir.dt.bfloat16
AF = mybir.ActivationFunctionType
ALU = mybir.AluOpType
AX = mybir.AxisListType

D = 1024
T = 8192
NT = T // 128
EPS = 1e-6
NQH, NKV, DH = 16, 4, 64
DI = 2048
NH = 32
NG = 4
DS = 128
CONV = 3072
DFF = 4096
import os
PAD = 2
N_WARM = int(os.environ.get('N_WARM', '1'))
DBG_SKIP = int(os.environ.get('DBG_SKIP', '0'))
CACHE_CONV = bool(int(os.environ.get('CACHE_CONV', '1')))
SSD_NCH = int(os.environ.get('SSD_NCH', '64'))
SSD_PASSES = int(os.environ.get('SSD_PASSES', '2'))

QPERM = []
for _p in range(8):
    _a = (_p // 4) * 8 + (_p % 4)
    QPERM += [_a, _a + 4]


class Buf:
    __slots__ = ("name", "w", "r", "dsem", "dval", "psum")

    def __init__(self, name, psum=False):
        self.name = name
        self.psum = psum
        self.w = None
        self.r = {}
        self.dsem = None
        self.dval = 0


class V:
    __slots__ = ("ap", "bufs")

    def __init__(self, ap, bufs):
        self.ap = ap
        if isinstance(bufs, Buf):
            bufs = (bufs,)
        self.bufs = tuple(bufs) if bufs else ()

    def __getitem__(self, i):
        return V(self.ap[i], self.bufs)

    def re(self, pat, **kw):
        return V(self.ap.rearrange(pat, **kw), self.bufs)

    def bc(self, shape):
        return V(self.ap.to_broadcast(list(shape)), self.bufs)

    def cast(self, dt):
        return V(self.ap.bitcast(dt), self.bufs)


SAME_ENGINE_WAIT = {"pe": False, "act": True, "dve": True, "pool": True, "sp": False}


class Eng:
    def __init__(self, K, name, e):
        self.K = K
        self.name = name
        self.e = e
        self.sem = K.nc.alloc_semaphore("sem_" + name)
        self.cnt = 0
        self.seenE = {}
        self.seenD = {}

    def _wait(self, tok):
        if tok is None:
            return
        if tok[0] == "E":
            _, en, c = tok
            if self.seenE.get(en, 0) >= c:
                return
            if en == self.name and not SAME_ENGINE_WAIT[en]:
                return
            self.e.wait_ge(self.K.eng[en].sem, c)
            self.seenE[en] = c
        else:
            b = tok[1]
            if b.dsem is None:
                return
            v = b.dval
            if self.seenD.get(id(b.dsem), 0) >= v:
                return
            self.e.wait_ge(b.dsem, v)
            self.seenD[id(b.dsem)] = v

    def _deps(self, outs, ins):
        for v in ins:
            for b in v.bufs:
                self._wait(b.w)
                if b.psum:
                    for k, t in list(b.r.items()):
                        if k != self.name:
                            self._wait(t)
        for v in outs:
            for b in v.bufs:
                self._wait(b.w)
                for t in list(b.r.values()):
                    self._wait(t)

    def do(self, fn, outs, ins, inc=True):
        outs = [v for v in outs if v is not None]
        ins = [v for v in ins if v is not None]
        self._deps(outs, ins)
        inst = fn()
        tok = ("E", self.name, self.cnt + 1)
        if inc:
            self.cnt += 1
            inst.then_inc(self.sem, 1)
        for v in ins:
            for b in v.bufs:
                b.r[self.name] = tok
        for v in outs:
            for b in v.bufs:
                b.w = tok
                b.r = {}
        return inst

    def dma(self, out, in_, slot, **kw):
        self._deps([out], [in_])
        if slot.dsem is None:
            if self.K.sempool:
                slot.dsem, slot.dval = self.K.sempool.pop()
            else:
                slot.dsem = self.K.nc.alloc_semaphore("dsem_%d" % self.K.uid)
                self.K.uid += 1
                slot.dval = 0
            self.K.dma_bufs.append(slot)
        slot.dval += 16
        self.e.dma_start(out=out.ap, in_=in_.ap, **kw).then_inc(slot.dsem, 16)
        tok = ("D", slot)
        for b in in_.bufs:
            b.r[("D", id(slot))] = tok
        for b in out.bufs:
            b.w = tok
            b.r = {}


class Kern:
    def __init__(self):
        nc = bass.Bass("TRN2", target_bir_lowering=False)
        self.nc = nc
        self.eng = {}
        self.dma_bufs = []
        self.sempool = []
        for name, e in (("pe", nc.tensor), ("act", nc.scalar), ("dve", nc.vector),
                        ("pool", nc.gpsimd), ("sp", nc.sync)):
            self.eng[name] = Eng(self, name, e)
        self.pe, self.act, self.dve, self.pool, self.sp = (self.eng[n] for n in
                                                           ("pe", "act", "dve", "pool", "sp"))
        self.barsem = nc.alloc_semaphore("barsem")
        self.barcnt = 0
        self.uid = 0
        self.psum = nc.alloc_psum_tensor("psum_all", [128, 8 * 512], F32).ap()
        self.pbuf = [Buf("psb%d" % i, psum=True) for i in range(8)]

    def sb(self, stack, name, shape, dt, nbuf=None):
        def one(nm):
            h = stack.enter_context(self.nc.sbuf_tensor("sb_" + nm, list(shape), dt))
            return V(h.ap(), Buf(nm))
        if nbuf is None:
            return one(name)
        return [one("%s_%d" % (name, i)) for i in range(nbuf)]

    def ps(self, bank, nbanks=1, dt=F32):
        ap = self.psum[:, bank * 512:(bank + nbanks) * 512]
        if dt != F32:
            ap = ap.bitcast(dt)
        return V(ap, self.pbuf[bank:bank + nbanks])

    def dram(self, name, shape, dt, kind=None):
        if kind:
            return self.nc.dram_tensor(name, list(shape), dt, kind=kind).ap()
        return self.nc.dram_tensor(name, list(shape), dt).ap()

    def barrier(self):
        sp = self.sp
        for en, E in self.eng.items():
            if en != "sp" and E.cnt > 0:
                sp._wait(("E", en, E.cnt))
        for b in self.dma_bufs:
            if b.dval > 0:
                sp._wait(("D", b))
        self.barcnt += 1
        sp.e.sem_inc(self.barsem, 1)
        for en, E in self.eng.items():
            if en != "sp":
                E.e.wait_ge(self.barsem, self.barcnt)
                for en2, E2 in self.eng.items():
                    E.seenE[en2] = E2.cnt
                for b in self.dma_bufs:
                    E.seenD[id(b.dsem)] = b.dval

    def mark(self):
        return len(self.dma_bufs)

    def recycle(self, n0):
        while len(self.dma_bufs) > n0:
            b = self.dma_bufs.pop()
            self.sempool.append((b.dsem, b.dval))
            b.dsem = None

    def mm(self, out, lhsT, rhs, start=True, stop=True, inc=True):
        return self.pe.do(lambda: self.nc.tensor.matmul(out.ap, lhsT=lhsT.ap, rhs=rhs.ap,
                                                        start=start, stop=stop),
                          [out], [lhsT, rhs], inc=inc)

    def tr(self, out, in_, ident, inc=True):
        return self.pe.do(lambda: self.nc.tensor.transpose(out.ap, in_.ap, ident.ap),
                          [out], [in_, ident], inc=inc)

    def actf(self, out, in_, func, bias=None, scale=None, accum=None):
        kw = {}
        ins = [in_]
        if bias is not None:
            if isinstance(bias, V):
                kw["bias"] = bias.ap
                ins.append(bias)
            else:
                kw["bias"] = bias
        if scale is not None:
            if isinstance(scale, V):
                kw["scale"] = scale.ap
                ins.append(scale)
            else:
                kw["scale"] = scale
        outs = [out]
        if accum is not None:
            kw["accum_out"] = accum.ap
            outs.append(accum)
        return self.act.do(lambda: self.nc.scalar.activation(out=out.ap, in_=in_.ap, func=func, **kw),
                           outs, ins)

    def _veng(self, eng):
        return (self.dve, self.nc.vector) if eng == "dve" else (self.pool, self.nc.gpsimd)

    def tt(self, out, in0, in1, op, eng="dve"):
        E, e = self._veng(eng)
        return E.do(lambda: e.tensor_tensor(out=out.ap, in0=in0.ap, in1=in1.ap, op=op),
                    [out], [in0, in1])

    def ts(self, out, in0, s1, op0, s2=None, op1=None, eng="dve", accum=None):
        E, e = self._veng(eng)
        ins = [in0]
        a1 = s1
        if isinstance(s1, V):
            a1 = s1.ap
            ins.append(s1)
        a2 = s2
        if isinstance(s2, V):
            a2 = s2.ap
            ins.append(s2)
        kw = {}
        if op1 is not None:
            kw["op1"] = op1
        outs = [out]
        if accum is not None:
            kw["accum_out"] = accum.ap
            outs.append(accum)
        return E.do(lambda: e.tensor_scalar(out=out.ap, in0=in0.ap, scalar1=a1, scalar2=a2, op0=op0, **kw),
                    outs, ins)

    def stt(self, out, in0, scalar, in1, op0, op1):
        ins = [in0, in1]
        a = scalar
        if isinstance(scalar, V):
            a = scalar.ap
            ins.append(scalar)
        return self.dve.do(lambda: self.nc.vector.scalar_tensor_tensor(out=out.ap, in0=in0.ap, scalar=a,
                                                                       in1=in1.ap, op0=op0, op1=op1),
                           [out], ins)

    def cp(self, out, in_, eng="dve"):
        if eng == "act":
            return self.act.do(lambda: self.nc.scalar.copy(out=out.ap, in_=in_.ap), [out], [in_])
        E, e = self._veng(eng)
        return E.do(lambda: e.tensor_copy(out=out.ap, in_=in_.ap), [out], [in_])

    def red(self, out, in_, op=ALU.add, axis=AX.X):
        return self.dve.do(lambda: self.nc.vector.tensor_reduce(out=out.ap, in_=in_.ap, axis=axis, op=op),
                           [out], [in_])

    def recip(self, out, in_):
        return self.dve.do(lambda: self.nc.vector.reciprocal(out=out.ap, in_=in_.ap), [out], [in_])

    def memset(self, out, val, eng="dve"):
        E, e = self._veng(eng)
        return E.do(lambda: e.memset(out.ap, val), [out], [])

    def load(self, out, in_ap, q="sp", **kw):
        E = self.eng[q]
        E.dma(out, V(in_ap, ()), out.bufs[0], **kw)

    def store(self, out_ap, in_, q="sp", **kw):
        E = self.eng[q]
        E.dma(V(out_ap, ()), in_, in_.bufs[0], **kw)


def skewed(n, stages):
    k = len(stages)
    for step in range(n + k - 1):
        for si, fn in enumerate(stages):
            t = step - si
            if 0 <= t < n:
                fn(t)


def load_w_bf16(K, stack, name, dview, J, N, chunk=2048):
    w = K.sb(stack, name, [128, J, N], BF16)
    n0 = K.mark()
    with ExitStack() as st:
        stg = K.sb(st, name + "_stg", [128, chunk], F32, nbuf=3)
        i = 0
        for j in range(J):
            for c0 in range(0, N, chunk):
                c1 = min(N, c0 + chunk)
                s = stg[i % 3]
                K.load(s[:, 0:c1 - c0], dview[:, j, c0:c1])
                K.cp(w[:, j, c0:c1], s[:, 0:c1 - c0], eng=("dve", "pool", "act")[i % 3])
                i += 1
        K.barrier()
        K.recycle(n0)
    return w


def build(debug=(), phases=("attn", "ssd", "merge", "mlp")):
    K = Kern()
    nc = K.nc
    dbg = set(debug)

    def scratch(name, shape, dt):
        return K.dram(name, shape, dt, kind="ExternalOutput" if name in dbg else None)

    x_d = K.dram("x", [T, D], F32, "ExternalInput")
    c_d = K.dram("c", [128, 8], F32, "ExternalInput")
    w_ada_d = K.dram("w_ada", [D, 6 * D], F32, "ExternalInput")
    b_ada_d = K.dram("b_ada", [1, 6 * D], F32, "ExternalInput")
    n1w_d = K.dram("norm1_w", [1, D], F32, "ExternalInput")
    n2w_d = K.dram("norm2_w", [1, D], F32, "ExternalInput")
    ident_d = K.dram("ident", [128, 128], F32, "ExternalInput")
    wqkv_d = K.dram("w_qkv", [D, 1536], F32, "ExternalInput")
    wqk_d = K.dram("wqk", [1, 1280], F32, "ExternalInput")
    cs_d = K.dram("cs", [T, 64], F32, "ExternalInput")
    wao_d = K.dram("w_ao", [D, D], F32, "ExternalInput")
    wxbc_d = K.dram("w_xbc", [D, CONV], F32, "ExternalInput")
    wz_d = K.dram("w_z", [D, DI], F32, "ExternalInput")
    wdt_d = K.dram("w_dt", [D, 64], F32, "ExternalInput")
    wg_d = K.dram("w_g", [D, 2 * D], F32, "ExternalInput")
    convw_d = K.dram("convw", [128, 24, 5], F32, "ExternalInput")
    convb_d = K.dram("convb", [128, 24], F32, "ExternalInput")
    dtb_d = K.dram("dtb", [1, 64], F32, "ExternalInput")
    alog_d = K.dram("alog", [1, 64], F32, "ExternalInput")
    dfull_d = K.dram("dfull", [1, DI], F32, "ExternalInput")
    snw_d = K.dram("snw", [1, DI], F32, "ExternalInput")
    wso_d = K.dram("w_so", [DI, D], F32, "ExternalInput")
    masks_d = K.dram("masks", [128, 5, 128], F32, "ExternalInput")
    wo_d = K.dram("w_o", [D, D], F32, "ExternalInput")
    w1_d = K.dram("w_1", [D, DFF], F32, "ExternalInput")
    w2_d = K.dram("w_2", [DFF, D], F32, "ExternalInput")
    out_d = K.dram("out", [T, D], F32, "ExternalOutput")

    hT_s = scratch("hT_s", [8, 128, T + 2 * PAD], BF16)
    mod_s = scratch("mod_s", [128, 4 * D], F32)
    qT_s = scratch("qT_s", [8, 128, T], BF16)
    ao_s = scratch("ao_s", [T, D], F32)
    y_s = scratch("y_s", [T, DI], F32)
    so_s = scratch("so_s", [T, D], F32)
    x1_s = scratch("x1_s", [T, D], F32)
    h2T_s = scratch("h2T_s", [8, 128, T], BF16)
    CH = 128 * CONV

    def _flat_bf16(ap2d):
        return ap2d.bitcast(BF16).rearrange("a b -> (a b)") if ap2d.dtype != BF16 else ap2d

    x1_flat = x1_s.bitcast(BF16).rearrange("a b -> (a b)")
    qT_flat = qT_s.rearrange("j p t -> (j p t)")
    h2_flat = h2T_s.rearrange("j p t -> (j p t)")

    def xc_view(c):
        if c < 42:
            fl, i = x1_flat, c
        elif c < 63:
            fl, i = qT_flat, c - 42
        else:
            fl, i = h2_flat, c - 63
        return fl[i * CH:(i + 1) * CH].rearrange("(p f) -> p f", f=CONV)

    dtr_flat = h2T_s.bitcast(F32).rearrange("j p t -> (j p t)")

    def dtr_view(c):
        off = 2 * CH // 2 + c * 128 * 64
        return dtr_flat[off:off + 128 * 64].rearrange("(p f) -> p f", f=64)

    with ExitStack() as glob:
        identf = K.sb(glob, "identf", [128, 128], F32)
        identb = K.sb(glob, "identb", [128, 128], BF16)
        K.load(identf, ident_d)
        K.cp(identb, identf)
        n_glob = K.mark()

        with ExitStack() as ph:
            modb = K.sb(ph, "modb", [128, 6 * D], F32)
            g1 = K.sb(ph, "g1", [128, D], F32)
            n_p0 = K.mark()
            with ExitStack() as p0:
                bab = K.sb(p0, "bab", [128, 6 * D], F32)
                K.load(bab, b_ada_d.partition_broadcast(128) if False else b_ada_d.to_broadcast([128, 6 * D]))
                n1b = K.sb(p0, "n1b", [128, D], F32)
                n2b = K.sb(p0, "n2b", [128, D], F32)
                K.load(n1b, n1w_d.to_broadcast([128, D]))
                K.load(n2b, n2w_d.to_broadcast([128, D]))
                csb = K.sb(p0, "csb", [128, 8, 1], F32)
                K.load(csb, c_d.rearrange("p (j o) -> p j o", o=1))
                scs = K.sb(p0, "scs", [128, 8, 1], F32)
                K.actf(scs, csb, AF.Silu)
                lhsc = K.sb(p0, "lhsc", [128, 8, 128], F32)
                K.cp(lhsc, scs.bc([128, 8, 128]))
                wa = K.sb(p0, "wa", [128, 8, 512], F32, nbuf=2)
                w_ada_v = w_ada_d.rearrange("(j p) n -> p j n", p=128)
                for nn in range(12):
                    w = wa[nn % 2]
                    K.load(w, w_ada_v[:, :, nn * 512:(nn + 1) * 512])
                    pst = K.ps(nn % 2)
                    for j in range(8):
                        K.mm(pst, lhsc[:, j, :], w[:, j, :], start=(j == 0), stop=(j == 7), inc=(j == 7))
                    K.tt(modb[:, nn * 512:(nn + 1) * 512], pst, bab[:, nn * 512:(nn + 1) * 512], ALU.add)
                K.stt(g1, modb[:, D:2 * D], 1.0, n1b, ALU.add, ALU.mult)
                g2 = K.sb(p0, "g2", [128, D], F32)
                K.stt(g2, modb[:, 4 * D:5 * D], 1.0, n2b, ALU.add, ALU.mult)
                K.store(mod_s[:, 0:D], g2)
                K.store(mod_s[:, D:2 * D], modb[:, 3 * D:4 * D])
                K.store(mod_s[:, 2 * D:3 * D], modb[:, 2 * D:3 * D])
                K.store(mod_s[:, 3 * D:4 * D], modb[:, 5 * D:6 * D])
                K.barrier()
                K.recycle(n_p0)
            shift1 = modb[:, 0:D]

            xt = K.sb(ph, "xt", [128, D], F32, nbuf=3)
            junk = K.sb(ph, "junk", [128, D], BF16)
            t1 = K.sb(ph, "t1", [128, D], F32, nbuf=2)
            t2 = K.sb(ph, "t2", [128, D], F32, nbuf=2)
            hb = K.sb(ph, "hb", [128, D], BF16, nbuf=2)
            ss = K.sb(ph, "ss", [128, 1], F32, nbuf=2)
            rt = K.sb(ph, "rt", [128, 1], F32, nbuf=2)
            rs = K.sb(ph, "rs", [128, 1], F32, nbuf=2)
            hTs = K.sb(ph, "hTs", [128, 8, 512], BF16, nbuf=2)
            zpad = K.sb(ph, "zpad", [128, 8, PAD], BF16)
            K.memset(zpad, 0.0)
            hT_v = hT_s.rearrange("j p t -> p j t")
            K.store(hT_v[:, :, 0:PAD], zpad)
            K.store(hT_v[:, :, T + PAD:T + 2 * PAD], zpad)
            PF = 2
            for it in range(NT + PF):
                if it < NT:
                    K.load(xt[it % 3], x_d[it * 128:(it + 1) * 128, :])
                t = it - PF
                if t < 0:
                    continue
                blk, s = t // 4, t % 4
                xv = xt[t % 3]
                K.actf(junk, xv, AF.Square, accum=ss[t % 2])
                K.actf(rt[t % 2], ss[t % 2], AF.Sqrt, bias=EPS, scale=1.0 / D)
                K.recip(rs[t % 2], rt[t % 2])
                K.actf(t1[t % 2], xv, AF.Copy, scale=rs[t % 2])
                K.tt(t2[t % 2], t1[t % 2], g1, ALU.mult, eng="pool")
                K.tt(hb[t % 2], t2[t % 2], shift1, ALU.add)
                pbase = (blk % 2) * 4
                for j in range(8):
                    pv = K.ps(pbase + j // 2, dt=BF16)
                    K.tr(pv[:, (j % 2) * 512 + s * 128:(j % 2) * 512 + (s + 1) * 128],
                         hb[t % 2][:, j * 128:(j + 1) * 128], identb)
                if s == 3:
                    hts = hTs[blk % 2]
                    for b4 in range(4):
                        K.cp(hts[:, 2 * b4:2 * b4 + 2, :].re("p a t -> p (a t)"), K.ps(pbase + b4, dt=BF16),
                             eng=("dve" if b4 % 2 == 0 else "act"))
                    K.store(hT_v[:, :, PAD + blk * 512:PAD + (blk + 1) * 512], hts)
            K.barrier()
            K.recycle(n_glob)

        hT_v = hT_s.rearrange("j p t -> p j t")
        if "attn" in phases:
          with ExitStack() as pa:
            KT = K.sb(pa, "KT", [128, 2, T], BF16)
            VA = K.sb(pa, "VA", [128, NT, 4, 128], BF16)
            VA5 = VA.re("p t (s h) c -> p t s h c", h=2)
            K.memset(VA5[:, :, :, 0, 64:128], 1.0, eng="pool")
            K.memset(VA5[:, :, :, 1, 0:64], 1.0, eng="pool")
            with ExitStack() as p2:
                wqkv = load_w_bf16(K, p2, "wqkv", wqkv_d.rearrange("(j p) n -> p j n", p=128), 8, 1536)
                csl = K.sb(p2, "csl", [128, 64], F32, nbuf=3)
                wqkb = K.sb(p2, "wqkb", [128, 1280], F32)
                K.load(wqkb, wqk_d.to_broadcast([128, 1280]))
                hTl = K.sb(p2, "hTl", [128, 8, 512], BF16, nbuf=2)
                sq = K.sb(p2, "sq", [128, 20, 64], F32)
                ssq = K.sb(p2, "ssq", [128, 20, 1], F32, nbuf=2)
                rtq = K.sb(p2, "rtq", [128, 20, 1], F32, nbuf=2)
                rsq = K.sb(p2, "rsq", [128, 20, 1], F32, nbuf=2)
                qn = K.sb(p2, "qn", [128, 20, 64], F32, nbuf=2)
                qw = qn
                tA = K.sb(p2, "tA", [128, 20, 64], F32)
                tB = K.sb(p2, "tB", [128, 20, 64], F32)
                qr = K.sb(p2, "qr", [128, 20, 64], BF16, nbuf=2)
                qTst = K.sb(p2, "qTst", [128, 8, 512], BF16, nbuf=2)
                qT_v = qT_s.rearrange("j p t -> p j t")
                def qA(t):
                    blk, s = t // 4, t % 4
                    hl = hTl[blk % 2]
                    if s == 0:
                        K.load(hl, hT_v[:, :, PAD + blk * 512:PAD + (blk + 1) * 512])
                    pb = (t % 2) * 3
                    csb = csl[t % 3]
                    K.load(csb, cs_d[t * 128:(t + 1) * 128, :])
                    for n in range(3):
                        for j in range(8):
                            K.mm(K.ps(pb + n), hl[:, j, s * 128:(s + 1) * 128],
                                 wqkv[:, j, n * 512:(n + 1) * 512], start=(j == 0), stop=(j == 7), inc=(j == 7))
                    pqk = K.ps(pb, 3)[:, 0:1280].re("p (h d) -> p h d", d=64)
                    pv_ = K.ps(pb, 3)[:, 1280:1536]
                    pv4 = pv_.re("p (s h d) -> p s h d", h=2, d=64)
                    K.cp(VA5[:, t, :, 0, 0:64], pv4[:, :, 0, :], eng="act")
                    K.cp(VA5[:, t, :, 1, 64:128], pv4[:, :, 1, :], eng="act")
                    K.actf(sq, pqk, AF.Square)
                    K.red(ssq[t % 2], sq)
                    K.actf(rtq[t % 2], ssq[t % 2], AF.Sqrt, bias=EPS, scale=1.0 / DH)
                    K.recip(rsq[t % 2], rtq[t % 2])
                    K.tt(qn[t % 2], pqk, rsq[t % 2].bc([128, 20, 64]), ALU.mult)
                    K.tt(qw[t % 2], qn[t % 2], wqkb.re("p (h d) -> p h d", d=64), ALU.mult, eng="pool")
                    q4 = qw[t % 2].re("p h (a b e) -> p h a b e", a=2, b=2)
                    A4 = tA.re("p h (a b e) -> p h a b e", a=2, b=2)
                    B4 = tB.re("p h (a b e) -> p h a b e", a=2, b=2)
                    o4 = qr[t % 2].re("p h (a b e) -> p h a b e", a=2, b=2)
                    for a in range(2):
                        cosv = csb[:, a * 16:(a + 1) * 16].re("p (h b e) -> p h b e", h=1, b=1).bc([128, 20, 2, 16])
                        sinv = csb[:, 32 + a * 16:32 + (a + 1) * 16].re("p (h b e) -> p h b e", h=1, b=1).bc([128, 20, 2, 16])
                        K.tt(A4[:, :, a], q4[:, :, a], cosv, ALU.mult)
                        K.tt(B4[:, :, a], q4[:, :, a], sinv, ALU.mult, eng="pool")
                        K.tt(o4[:, :, a, 0], A4[:, :, a, 0], B4[:, :, a, 1], ALU.subtract)
                        K.tt(o4[:, :, a, 1], A4[:, :, a, 1], B4[:, :, a, 0], ALU.add, eng="pool")

                def qB(t):
                    blk, s = t // 4, t % 4
                    qst = qTst[blk % 2]
                    qrf = qr[t % 2].re("p h d -> p (h d)")
                    pq = K.ps(6, dt=BF16)
                    pk = K.ps(7, dt=BF16)
                    for p in range(8):
                        K.tr(pq[:, p * 128:(p + 1) * 128], qrf[:, p * 128:(p + 1) * 128], identb)
                    for p in range(2):
                        K.tr(pk[:, p * 128:(p + 1) * 128], qrf[:, 1024 + p * 128:1024 + (p + 1) * 128], identb)
                    K.cp(qst[:, :, s * 128:(s + 1) * 128], pq.re("p (j t) -> p j t", t=128), eng="act")
                    K.cp(KT[:, :, t * 128:(t + 1) * 128], pk[:, 0:256].re("p (j t) -> p j t", t=128))
                    if s == 3:
                        K.store(qT_v[:, :, blk * 512:(blk + 1) * 512], qst)

                skewed(NT, [qA, qB])
                K.barrier()
                K.recycle(n_glob)

            with ExitStack() as p3:
                wao = load_w_bf16(K, p3, "wao", wao_d.rearrange("(j p) n -> p j n", p=128), 8, D)
                qbd = K.sb(p3, "qbd", [128, 8, 2, 512], BF16, nbuf=2)
                for qq in qbd:
                    K.memset(qq, 0.0, eng="pool")
                PT = K.sb(p3, "PT", [128, 1024], BF16, nbuf=4)
                Osb = K.sb(p3, "Osb", [128, 512], F32, nbuf=2)
                rd = K.sb(p3, "rd", [128, 512], F32, nbuf=2)
                attnT = K.sb(p3, "attnT", [128, 8, 512], BF16, nbuf=2)
                aot = K.sb(p3, "aot", [128, D], F32, nbuf=2)
                qT_v = qT_s.rearrange("j p t -> p j t")
                NQB = T // 512

                def load_q(qb):
                    qq = qbd[qb % 2]
                    srcv = qT_v[:, :, qb * 512:(qb + 1) * 512].rearrange("p j (m t) -> p j m t", m=2)
                    for m in range(2):
                        K.load(qq[0:64, :, m, 0:256], srcv[0:64, :, m, :])
                        K.load(qq[64:128, :, m, 256:512], srcv[64:128, :, m, :])

                load_q(0)
                for qb in range(NQB):
                    if qb + 1 < NQB:
                        load_q(qb + 1)
                    ql = qbd[qb % 2]
                    aT = attnT[qb % 2]
                    NUq = 8 * NT
                    SKEW = 2
                    for n in range(NUq + SKEW):
                        if n < NUq:
                            p, kt = divmod(n, NT)
                            slot = p // 4
                            gi = qb * NUq + n
                            Sb = K.ps(2 + 2 * (gi % 3), 2)
                            for m in range(2):
                                K.mm(Sb[:, m * 512:(m + 1) * 512], KT[:, slot, kt * 128:(kt + 1) * 128],
                                     ql[:, p, m, :], inc=(m == 1))
                            K.actf(PT[gi % 4], Sb, AF.Exp, scale=DH ** -0.5)
                        if n >= SKEW:
                            n2 = n - SKEW
                            p, kt = divmod(n2, NT)
                            slot = p // 4
                            gi2 = qb * NUq + n2
                            P4 = PT[gi2 % 4].re("p (m h t) -> p m h t", m=2, h=2)
                            for half in range(2):
                                g = slot * 2 + half
                                K.mm(K.ps(half), VA[:, kt, g, :], P4[:, :, half, :],
                                     start=(kt == 0), stop=(kt == NT - 1))
                            if kt == NT - 1:
                                for half in range(2):
                                    K.cp(Osb[half], K.ps(half), eng="act")
                                    if half == 0:
                                        K.recip(rd[half][0:64, :], Osb[half][64:128, :])
                                        K.tt(aT[0:64, p, :], Osb[half][0:64, :], rd[half][0:64, :], ALU.mult)
                                    else:
                                        K.recip(rd[half][64:128, :], Osb[half][0:64, :])
                                        K.tt(aT[64:128, p, :], Osb[half][64:128, :], rd[half][64:128, :], ALU.mult)
                    for s in range(4):
                        for n in range(2):
                            for p in range(8):
                                K.mm(K.ps(6 + n), aT[:, p, s * 128:(s + 1) * 128], wao[:, p, n * 512:(n + 1) * 512],
                                     start=(p == 0), stop=(p == 7), inc=(p == 7))
                        ao = aot[s % 2]
                        K.cp(ao, K.ps(6, 2))
                        K.store(ao_s[qb * 512 + s * 128:qb * 512 + (s + 1) * 128, :], ao)
                K.barrier()
                K.recycle(n_glob)

        if "ssd" in phases:
          with ExitStack() as pss:
            wxbc = load_w_bf16(K, pss, "wxbc", wxbc_d.rearrange("(j p) n -> p j n", p=128), 8, CONV)
            wdt = load_w_bf16(K, pss, "wdt", wdt_d.rearrange("(j p) n -> p j n", p=128), 8, 64)
            msk = K.sb(pss, "msk", [128, 5, 128], F32)
            K.load(msk, masks_d)
            convw = K.sb(pss, "convw", [128, 24, 5], F32)
            K.load(convw, convw_d)
            convb = K.sb(pss, "convb", [128, 24], F32)
            K.load(convb, convb_d)
            dtb = K.sb(pss, "dtb", [128, 64], F32)
            K.load(dtb, dtb_d.to_broadcast([128, 64]))
            Ab = K.sb(pss, "Ab", [128, 64], F32)
            K.load(Ab, alog_d.to_broadcast([128, 64]))
            K.actf(Ab, Ab, AF.Exp)
            K.ts(Ab, Ab, -1.0, ALU.mult)
            Db = K.sb(pss, "Db", [128, DI], F32)
            K.load(Db, dfull_d.to_broadcast([128, DI]))
            diag = K.sb(pss, "diag", [128, 24, 5, 128], BF16)
            for ct in range(24):
                for k in range(5):
                    K.ts(diag[:, ct, k, :], identf, convw[:, ct, k:k + 1], ALU.mult)
            hw = K.sb(pss, "hw", [128, 8, 512 + 2 * PAD], BF16, nbuf=2)
            xbcT = K.sb(pss, "xbcT", [128, 24, 128 + 2 * PAD], BF16)
            xcTb = K.sb(pss, "xcT", [128, 24, 128], BF16, nbuf=2)
            dtrb = K.sb(pss, "dtrb", [128, 64], F32, nbuf=2)
            Btm = K.sb(pss, "Btm", [128, 512], BF16)
            st = K.sb(pss, "st", [128, NH, 64], F32)
            stbf = K.sb(pss, "stbf", [128, DI], BF16)
            rhi = K.sb(pss, "rhi", [128, NH, 128], BF16)
            rlo = K.sb(pss, "rlo", [128, NH, 128], BF16)
            mskb = K.sb(pss, "mskb", [128, 5, 128], BF16)
            K.cp(mskb, msk)
            Ebh = K.sb(pss, "Eb", [128, 16, 128], BF16, nbuf=2)
            CBm = K.sb(pss, "CBm", [128, 4, 128], BF16)
            Xb = K.sb(pss, "Xb", [128, DI], BF16)
            Xd = K.sb(pss, "Xd", [128, DI], BF16)
            ytmp = K.sb(pss, "ytmp", [128, D], F32)
            yacc = K.sb(pss, "yacc", [128, D], F32, nbuf=1) * 2
            yfin = K.sb(pss, "yfin", [128, DI], F32, nbuf=2)
            smb = [{n: K.sb(pss, "sm%d_%s" % (i, n), [128, 32], F32) for n in
                    ("x", "ab", "e", "l", "dt", "a", "acs", "expA", "dd", "dec", "cd", "w2")} for i in range(2)]
            smh = [(K.sb(pss, "ahi%d" % i, [128, 32], BF16), K.sb(pss, "alo%d" % i, [128, 32], BF16)) for i in range(2)]

            for d in range(SSD_PASSES):
                Ud = msk[:, d, :]
                Lsd = msk[:, 2 + d, :]
                ones = msk[:, 4, :]
                K.memset(st, 0.0)
                K.memset(stbf, 0.0, eng="pool")
                order = list(range(NT))[:SSD_NCH]
                if d == 1:
                    order = order[::-1]
                for ci, c in enumerate(order):
                    grp = c // 4
                    recompute = (d == 0) or not CACHE_CONV
                    if ci % 4 == 0 and recompute:
                        K.load(hw[(ci // 4) % 2], hT_v[:, :, grp * 512:grp * 512 + 512 + 2 * PAD])
                    hwv = hw[(ci // 4) % 2]
                    o = (c % 4) * 128
                    yf = yfin[ci % 2]
                    if d == 1:
                        K.load(yf, y_s[c * 128:(c + 1) * 128, :])
                    xcT = xcTb[ci % 2]
                    sm = smb[ci % 2]
                    p7 = K.ps(7)
                    dtr = dtrb[ci % 2]
                    if recompute:
                        for j in range(8):
                            K.mm(p7[:, 0:64], hwv[:, j, o + PAD:o + PAD + 128], wdt[:, j, :], start=(j == 0), stop=(j == 7), inc=(j == 7))
                        K.cp(dtr, p7[:, 0:64])
                        if CACHE_CONV:
                            K.store(dtr_view(c), dtr)
                    else:
                        if ci == 0:
                            K.load(xcTb[0].re("p a t -> p (a t)"), xc_view(order[0]))
                            K.load(dtrb[0], dtr_view(order[0]))
                        if ci + 1 < len(order):
                            K.load(xcTb[(ci + 1) % 2].re("p a t -> p (a t)"), xc_view(order[ci + 1]))
                            K.load(dtrb[(ci + 1) % 2], dtr_view(order[ci + 1]))
                    K.tt(sm["x"], dtr[:, d * 32:(d + 1) * 32], dtb[:, d * 32:(d + 1) * 32], ALU.add)
                    K.stt(sm["ab"], sm["x"], -1.0, sm["x"], ALU.mult, ALU.max)
                    K.actf(sm["e"], sm["ab"], AF.Exp, scale=-1.0)
                    K.actf(sm["l"], sm["e"], AF.Ln, bias=1.0)
                    K.stt(sm["dt"], sm["x"], 0.0, sm["l"], ALU.max, ALU.add)
                    K.tt(sm["a"], sm["dt"], Ab[:, d * 32:(d + 1) * 32], ALU.mult)
                    for w in (range(8) if recompute else ()):
                        bank = 4 + w % 3
                        for i in range(3):
                            ct = 3 * w + i
                            for j in range(8):
                                K.mm(K.ps(bank)[:, i * 132:(i + 1) * 132], wxbc[:, j, ct * 128:(ct + 1) * 128],
                                     hwv[:, j, o:o + 132], start=(j == 0), stop=(j == 7), inc=(j == 7))
                        K.cp(xbcT[:, 3 * w:3 * w + 3, :].re("p a t -> p (a t)"), K.ps(bank)[:, 0:396], eng="act")
                    K.mm(p7[:, 64:96], Ud, sm["a"])
                    K.mm(p7[:, 96:128], ones, sm["a"])
                    K.cp(sm["acs"], p7[:, 64:96])
                    K.actf(sm["expA"], p7[:, 64:96], AF.Exp)
                    K.tt(sm["dd"], p7[:, 96:128], sm["acs"], ALU.subtract)
                    K.actf(sm["dec"], sm["dd"], AF.Exp)
                    K.actf(sm["cd"], p7[:, 96:128], AF.Exp)
                    K.tt(sm["w2"], sm["dt"], sm["dec"], ALU.mult)
                    ahi, alo = smh[ci % 2]
                    K.cp(ahi, sm["a"])
                    K.tt(alo, sm["a"], ahi, ALU.subtract)
                    Ub3 = mskb[:, d, :].re("p (g l) -> p g l", g=1).bc([128, NH, 128])
                    K.tt(rhi, Ub3, ahi.re("p (h o) -> p h o", o=1).bc([128, NH, 128]), ALU.mult)
                    K.tt(rlo, Ub3, alo.re("p (h o) -> p h o", o=1).bc([128, NH, 128]), ALU.mult, eng="pool")
                    for ct in (range(24) if recompute else ()):
                        po = K.ps(ct % 4)[:, ((ct // 4) % 4) * 128:((ct // 4) % 4 + 1) * 128]
                        for k in range(5):
                            K.mm(po, diag[:, ct, k, :], xbcT[:, ct, k:k + 128], start=(k == 0), stop=(k == 4), inc=(k == 4))
                        K.actf(xcT[:, ct, :], po, AF.Silu, bias=convb[:, ct:ct + 1])
                    if recompute and CACHE_CONV:
                        K.store(xc_view(c), xcT.re("p a t -> p (a t)"))
                    xs_ps = K.ps(4, 2, dt=BF16)
                    for ct in range(16):
                        K.tr(xs_ps[:, ct * 128:(ct + 1) * 128], xcT[:, ct, :], identb)
                    b_ps = K.ps(6, dt=BF16)
                    for g in range(4):
                        K.tr(b_ps[:, g * 128:(g + 1) * 128], xcT[:, 16 + g, :], identb)
                    K.cp(Btm, b_ps[:, 0:512])
                    xs3 = xs_ps.re("p (h e) -> p h e", e=64)
                    K.tt(Xb.re("p (h e) -> p h e", e=64), xs3, sm["dt"].re("p (h o) -> p h o", o=1).bc([128, NH, 64]), ALU.mult)
                    K.tt(Xd.re("p (h e) -> p h e", e=64), xs3, sm["w2"].re("p (h o) -> p h o", o=1).bc([128, NH, 64]), ALU.mult)
                    if d == 0:
                        K.tt(yf, xs_ps, Db, ALU.mult)
                    p3 = K.ps(3)
                    for g in range(4):
                        K.mm(p3[:, g * 128:(g + 1) * 128], xcT[:, 16 + g, :], xcT[:, 20 + g, :])
                    K.tt(CBm, p3.re("p (g l) -> p g l", l=128), Ud.re("p (g l) -> p g l", g=1).bc([128, 4, 128]), ALU.mult)
                    Lsb = mskb[:, 2 + d, :]
                    for hq in range(8):
                        pd = K.ps(hq % 2)
                        K.mm(pd, Lsb, rhi[:, 4 * hq:4 * hq + 4, :].re("p h l -> p (h l)"), start=True, stop=False, inc=False)
                        K.mm(pd, Lsb, rlo[:, 4 * hq:4 * hq + 4, :].re("p h l -> p (h l)"), start=False, stop=True)
                        K.actf(Ebh[hq // 4][:, 4 * (hq % 4):4 * (hq % 4) + 4, :].re("p h l -> p (h l)"), pd, AF.Exp)
                        if hq % 2 == 1:
                            g = hq // 2
                            mtv = Ebh[g // 2][:, 8 * (g % 2):8 * (g % 2) + 8, :]
                            K.tt(mtv, mtv, CBm[:, g:g + 1, :].bc([128, 8, 128]), ALU.mult)
                    for hh in range(2):
                        ydg = K.ps(2, 2)
                        for h in range(16 * hh, 16 * hh + 16):
                            K.mm(ydg[:, (h % 16) * 64:(h % 16 + 1) * 64], Ebh[hh][:, h % 16, :], Xb[:, h * 64:(h + 1) * 64],
                                 inc=(h % 16 == 15))
                        yof = K.ps(6, 2)
                        for g in (2 * hh, 2 * hh + 1):
                            K.mm(yof[:, (g % 2) * 512:(g % 2 + 1) * 512], xcT[:, 20 + g, :], stbf[:, g * 512:(g + 1) * 512],
                                 inc=(g % 2 == 1))
                        K.tt(ytmp.re("p (h e) -> p h e", e=64), yof.re("p (h e) -> p h e", e=64),
                             sm["expA"][:, 16 * hh:16 * hh + 16].re("p (h o) -> p h o", o=1).bc([128, 16, 64]), ALU.mult)
                        ya = yacc[hh]
                        K.tt(ya, ytmp, ydg, ALU.add)
                        K.tt(yf[:, hh * D:(hh + 1) * D], ya, yf[:, hh * D:(hh + 1) * D], ALU.add, eng="pool")
                    K.store(y_s[c * 128:(c + 1) * 128, :], yf)
                    pst = K.ps(0, 4)
                    for g in range(4):
                        K.mm(pst[:, g * 512:(g + 1) * 512], Btm[:, g * 128:(g + 1) * 128], Xd[:, g * 512:(g + 1) * 512])
                    K.tt(st, st, sm["cd"].re("p (h o) -> p h o", o=1).bc([128, NH, 64]), ALU.mult, eng="pool")
                    K.tt(st.re("p h e -> p (h e)"), st.re("p h e -> p (h e)"), pst, ALU.add)
                    K.cp(stbf, st.re("p h e -> p (h e)"), eng="pool")
                K.barrier()
            K.recycle(n_glob)

          with ExitStack() as pc:
            wz = load_w_bf16(K, pc, "wz", wz_d.rearrange("(j p) n -> p j n", p=128), 8, DI)
            wso = load_w_bf16(K, pc, "wso", wso_d.rearrange("(j p) n -> p j n", p=128), 16, D)
            snw = K.sb(pc, "snw", [128, DI], F32)
            K.load(snw, snw_d.to_broadcast([128, DI]))
            hTl = K.sb(pc, "hTl4", [128, 8, 512], BF16, nbuf=2)
            yl = K.sb(pc, "yl", [128, DI], F32, nbuf=3)
            zs = K.sb(pc, "zs", [128, DI], F32)
            yz = K.sb(pc, "yz", [128, DI], F32)
            junk2 = K.sb(pc, "junk2", [128, DI], BF16)
            yn = K.sb(pc, "yn", [128, DI], BF16, nbuf=2)
            ssdT = K.sb(pc, "ssdT", [128, 16, 128], BF16, nbuf=2)
            sot = K.sb(pc, "sot", [128, D], F32, nbuf=2)
            s1 = K.sb(pc, "s1", [128, 1], F32, nbuf=2)
            s2 = K.sb(pc, "s2", [128, 1], F32, nbuf=2)
            s3 = K.sb(pc, "s3", [128, 1], F32, nbuf=2)
            def cA(t):
                blk, s = t // 4, t % 4
                if s == 0:
                    K.load(hTl[blk % 2], hT_v[:, :, PAD + blk * 512:PAD + (blk + 1) * 512])
                hl = hTl[blk % 2]
                K.load(yl[t % 3], y_s[t * 128:(t + 1) * 128, :])
                for n in range(4):
                    for j in range(8):
                        K.mm(K.ps(n), hl[:, j, s * 128:(s + 1) * 128], wz[:, j, n * 512:(n + 1) * 512],
                             start=(j == 0), stop=(j == 7), inc=(j == 7))
                K.actf(zs, K.ps(0, 4), AF.Silu)
                K.tt(yz, yl[t % 3], zs, ALU.mult)
                K.actf(junk2, yz, AF.Square, accum=s1[t % 2])
                K.actf(s2[t % 2], s1[t % 2], AF.Sqrt, bias=EPS, scale=1.0 / DI)
                K.recip(s3[t % 2], s2[t % 2])
                K.stt(yn[t % 2], yz, s3[t % 2], snw, ALU.mult, ALU.mult)

            def cB(t):
                pT = K.ps(4, 2, dt=BF16)
                for j in range(16):
                    K.tr(pT[:, j * 128:(j + 1) * 128], yn[t % 2][:, j * 128:(j + 1) * 128], identb)
                K.cp(ssdT[t % 2].re("p j t -> p (j t)"), pT)

            def cC(t):
                for n in range(2):
                    for j in range(16):
                        K.mm(K.ps(6 + n), ssdT[t % 2][:, j, :], wso[:, j, n * 512:(n + 1) * 512],
                             start=(j == 0), stop=(j == 15), inc=(j == 15))
                K.cp(sot[t % 2], K.ps(6, 2), eng="act")
                K.store(so_s[t * 128:(t + 1) * 128, :], sot[t % 2])

            skewed(min(NT, SSD_NCH), [cA, cB, cC])
            K.barrier()
            K.recycle(n_glob)

        if "merge" in phases:
          with ExitStack() as pm:
            wg = load_w_bf16(K, pm, "wg", wg_d.rearrange("(j p) n -> p j n", p=128), 8, 2 * D)
            wo = load_w_bf16(K, pm, "wo", wo_d.rearrange("(j p) n -> p j n", p=128), 8, D)
            g2 = K.sb(pm, "g2m", [128, D], F32)
            sh2 = K.sb(pm, "sh2", [128, D], F32)
            gt1 = K.sb(pm, "gt1", [128, D], F32)
            K.load(g2, mod_s[:, 0:D])
            K.load(sh2, mod_s[:, D:2 * D])
            K.load(gt1, mod_s[:, 2 * D:3 * D])
            hTl = K.sb(pm, "hTl5", [128, 8, 512], BF16, nbuf=2)
            aol = K.sb(pm, "aol", [128, D], F32, nbuf=2)
            sol = K.sb(pm, "sol", [128, D], F32, nbuf=2)
            xl = K.sb(pm, "xl", [128, D], F32, nbuf=3)
            sig = K.sb(pm, "sig", [128, 2 * D], F32)
            m1 = K.sb(pm, "m1", [128, D], F32)
            m2 = K.sb(pm, "m2", [128, D], F32)
            mg = K.sb(pm, "mg", [128, D], BF16, nbuf=2)
            mT = K.sb(pm, "mT", [128, 8, 128], BF16, nbuf=2)
            tq = K.sb(pm, "tq", [128, D], F32)
            x1 = K.sb(pm, "x1", [128, D], F32, nbuf=2)
            junk3 = K.sb(pm, "junk3", [128, D], BF16)
            u1 = K.sb(pm, "u1", [128, D], F32)
            u2 = K.sb(pm, "u2", [128, D], F32)
            h2 = K.sb(pm, "h2", [128, D], BF16, nbuf=2)
            h2Ts = K.sb(pm, "h2Ts", [128, 8, 512], BF16, nbuf=2)
            a1 = K.sb(pm, "a1", [128, 1], F32, nbuf=2)
            a2 = K.sb(pm, "a2", [128, 1], F32, nbuf=2)
            a3 = K.sb(pm, "a3", [128, 1], F32, nbuf=2)
            h2T_v = h2T_s.rearrange("j p t -> p j t")
            def mA(t):
                blk, s = t // 4, t % 4
                if s == 0:
                    K.load(hTl[blk % 2], hT_v[:, :, PAD + blk * 512:PAD + (blk + 1) * 512])
                hl = hTl[blk % 2]
                rows = slice(t * 128, (t + 1) * 128)
                K.load(aol[t % 2], ao_s[rows, :])
                K.load(sol[t % 2], so_s[rows, :])
                K.load(xl[t % 3], x_d[rows, :])
                for n in range(4):
                    for j in range(8):
                        K.mm(K.ps(n), hl[:, j, s * 128:(s + 1) * 128], wg[:, j, n * 512:(n + 1) * 512],
                             start=(j == 0), stop=(j == 7), inc=(j == 7))
                K.actf(sig, K.ps(0, 4), AF.Sigmoid)
                K.tt(m1, sig[:, 0:D], aol[t % 2], ALU.mult)
                K.tt(m2, sig[:, D:2 * D], sol[t % 2], ALU.mult, eng="pool")
                K.tt(mg[t % 2], m1, m2, ALU.add)

            def mB(t):
                pT = K.ps(4, dt=BF16)
                for j in range(8):
                    K.tr(pT[:, j * 128:(j + 1) * 128], mg[t % 2][:, j * 128:(j + 1) * 128], identb)
                K.cp(mT[t % 2].re("p j t -> p (j t)"), pT, eng="act")

            def mC(t):
                rows = slice(t * 128, (t + 1) * 128)
                for n in range(2):
                    for j in range(8):
                        K.mm(K.ps(5 + n), mT[t % 2][:, j, :], wo[:, j, n * 512:(n + 1) * 512],
                             start=(j == 0), stop=(j == 7), inc=(j == 7))
                K.tt(tq, K.ps(5, 2), gt1, ALU.mult)
                xx = x1[t % 2]
                K.tt(xx, tq, xl[t % 3], ALU.add, eng="pool")
                K.store(x1_s[rows, :], xx)
                K.actf(junk3, xx, AF.Square, accum=a1[t % 2])
                K.actf(a2[t % 2], a1[t % 2], AF.Sqrt, bias=EPS, scale=1.0 / D)
                K.recip(a3[t % 2], a2[t % 2])
                K.actf(u1, xx, AF.Copy, scale=a3[t % 2])
                K.tt(u2, u1, g2, ALU.mult, eng="pool")
                K.tt(h2[t % 2], u2, sh2, ALU.add)

            def mD(t):
                blk, s = t // 4, t % 4
                pT2 = K.ps(7, dt=BF16)
                for j in range(8):
                    K.tr(pT2[:, j * 128:(j + 1) * 128], h2[t % 2][:, j * 128:(j + 1) * 128], identb)
                K.cp(h2Ts[blk % 2][:, :, s * 128:(s + 1) * 128], pT2.re("p (j t) -> p j t", t=128))
                if s == 3:
                    K.store(h2T_v[:, :, blk * 512:(blk + 1) * 512], h2Ts[blk % 2])

            skewed(NT, [mA, mB, mC, mD])
            K.barrier()
            K.recycle(n_glob)

        if "mlp" in phases:
          with ExitStack() as pf:
            w1 = load_w_bf16(K, pf, "w1", w1_d.rearrange("(j p) n -> p j n", p=128), 8, DFF)
            w2 = load_w_bf16(K, pf, "w2", w2_d.rearrange("(j p) n -> p j n", p=128), 32, D)
            gt2 = K.sb(pf, "gt2", [128, D], F32)
            K.load(gt2, mod_s[:, 3 * D:4 * D])
            TB = 256
            h2l = K.sb(pf, "h2l", [128, 8, TB], BF16, nbuf=2)
            uT = K.sb(pf, "uT", [128, 32, TB], BF16)
            rl = K.sb(pf, "rl", [128, TB], F32, nbuf=2)
            x1l = K.sb(pf, "x1l", [128, D], F32, nbuf=2)
            tf = K.sb(pf, "tf", [128, D], F32)
            ot = K.sb(pf, "ot", [128, D], F32, nbuf=2)
            h2T_v = h2T_s.rearrange("j p t -> p j t")
            NB = T // TB
            K.load(h2l[0], h2T_v[:, :, 0:TB])
            for blk in range(NB):
                if blk + 1 < NB:
                    K.load(h2l[(blk + 1) % 2], h2T_v[:, :, (blk + 1) * TB:(blk + 2) * TB])
                hh = h2l[blk % 2]
                for fc in range(32):
                    pf_ = K.ps(fc % 2)[:, 0:TB]
                    for j in range(8):
                        K.mm(pf_, w1[:, j, fc * 128:(fc + 1) * 128], hh[:, j, :], start=(j == 0), stop=(j == 7), inc=(j == 7))
                    K.actf(rl[fc % 2], pf_, AF.Relu)
                    K.tt(uT[:, fc, :], rl[fc % 2], rl[fc % 2], ALU.mult, eng=("dve" if fc % 2 == 0 else "pool"))
                for s in range(TB // 128):
                    tt_ = blk * (TB // 128) + s
                    rows = slice(tt_ * 128, (tt_ + 1) * 128)
                    K.load(x1l[tt_ % 2], x1_s[rows, :])
                    pb = 2 + 2 * (tt_ % 2)
                    for n in range(2):
                        for fc in range(32):
                            K.mm(K.ps(pb + n), uT[:, fc, s * 128:(s + 1) * 128], w2[:, fc, n * 512:(n + 1) * 512],
                                 start=(fc == 0), stop=(fc == 31), inc=(fc == 31))
                    K.tt(tf, K.ps(pb, 2), gt2, ALU.mult)
                    K.tt(ot[tt_ % 2], tf, x1l[tt_ % 2], ALU.add, eng="pool")
                    K.store(out_d[rows, :], ot[tt_ % 2])
            K.barrier()

        K.barrier()
    return K


def _bf16(a):
    return np.asarray(a, dtype=np.float32).astype(ml_dtypes.bfloat16)


def make_inputs(inputs, b):
    m = {}
    m["x"] = np.ascontiguousarray(inputs["x"][b])
    m["c"] = np.ascontiguousarray(np.asarray(inputs["c"][b]).reshape(8, 128).T)
    m["w_ada"] = np.ascontiguousarray(inputs["w_ada"][0])
    m["b_ada"] = np.ascontiguousarray(inputs["b_ada"][0].reshape(1, -1))
    m["norm1_w"] = np.ascontiguousarray(inputs["norm1_w"][0].reshape(1, -1))
    m["norm2_w"] = np.ascontiguousarray(inputs["norm2_w"][0].reshape(1, -1))
    m["ident"] = np.eye(128, dtype=np.float32)
    w_in = inputs["w_in"][0]
    wq = w_in[:, 0:1024].reshape(D, 16, 64)[:, QPERM, :].reshape(D, 1024)
    m["w_qkv"] = np.ascontiguousarray(np.concatenate([wq, w_in[:, 1024:1536]], axis=1))
    m["wqk"] = np.ascontiguousarray(np.concatenate([np.tile(inputs["q_norm_w"][0], 16),
                                                      np.tile(inputs["k_norm_w"][0], 4)]).reshape(1, 1280))
    m["cs"] = rope_table()
    m["w_ao"] = np.ascontiguousarray(inputs["w_attn_out"][0].reshape(16, 64, D)[QPERM].reshape(D, D))
    m["w_xbc"] = np.ascontiguousarray(w_in[:, 1536:4608])
    m["w_z"] = np.ascontiguousarray(w_in[:, 4608:6656])
    m["w_dt"] = np.ascontiguousarray(w_in[:, 6656:6720])
    m["w_g"] = np.ascontiguousarray(w_in[:, 6720:8768])
    m["convw"] = np.ascontiguousarray(inputs["conv_w"][0].reshape(5, 24, 128).transpose(2, 1, 0))
    m["convb"] = np.ascontiguousarray(inputs["conv_b"][0].reshape(24, 128).T)
    m["dtb"] = np.ascontiguousarray(inputs["dt_bias"][0].reshape(1, 64))
    m["alog"] = np.ascontiguousarray(inputs["A_log"][0].reshape(1, 64))
    m["dfull"] = np.ascontiguousarray(np.repeat(inputs["ssd_D"][0], 64).reshape(1, DI))
    m["snw"] = np.ascontiguousarray(inputs["ssd_norm_w"][0].reshape(1, DI))
    m["w_so"] = np.ascontiguousarray(inputs["w_ssd_out"][0])
    i = np.arange(128)
    uf = (i[:, None] <= i[None, :]).astype(np.float32)
    lf = (i[:, None] > i[None, :]).astype(np.float32)
    m["masks"] = np.ascontiguousarray(np.stack([uf, uf.T, lf, lf.T, np.ones((128, 128), np.float32)], axis=1))
    m["w_o"] = np.ascontiguousarray(inputs["w_o"][0])
    m["w_1"] = np.ascontiguousarray(inputs["w_mlp1"][0])
    m["w_2"] = np.ascontiguousarray(inputs["w_mlp2"][0])
    return m


def rope_table():
    pos = np.arange(T)
    pr = (pos // 64).astype(np.float32)
    pc = (pos % 64).astype(np.float32)
    inv = (np.float32(10000.0) ** (-np.arange(0, 32, 2, dtype=np.float32) / np.float32(32))).astype(np.float32)
    ar = pr[:, None] * inv[None, :]
    ac = pc[:, None] * inv[None, :]
    return np.ascontiguousarray(np.concatenate([np.cos(ar), np.cos(ac), np.sin(ar), np.sin(ac)], axis=1).astype(np.float32))


def kernel(**inputs):
    inputs = {k: np.asarray(v) for k, v in inputs.items()}
    K = build()
    in_maps = [make_inputs(inputs, b) for b in range(8)]
    res = run_bass_kernel_spmd(K.nc, in_maps, core_ids=list(range(8)))
    return np.stack([np.asarray(r["out"]) for r in res.results], axis=0).astype(np.float32)
```

```python
import numpy as np
import ml_dtypes
from contextlib import ExitStack
import concourse.bass as bass
import concourse.mybir as mybir
from concourse.bass_utils import run_bass_kernel_spmd

F32, BF16 = mybir.dt.float32, mybir.dt.bfloat16
AF = mybir.ActivationFunctionType
ALU = mybir.AluOpType
AX = mybir.AxisListType

D = 1024
T = 8192
NT = T // 128
EPS = 1e-6
NQH, NKV, DH = 16, 4, 64
DI = 2048
NH = 32
NG = 4
DS = 128
CONV = 3072
DFF = 4096
import os
PAD = 2
N_WARM = int(os.environ.get('N_WARM', '1'))
DBG_SKIP = int(os.environ.get('DBG_SKIP', '0'))
CACHE_CONV = bool(int(os.environ.get('CACHE_CONV', '1')))
SSD_NCH = int(os.environ.get('SSD_NCH', '64'))
SSD_PASSES = int(os.environ.get('SSD_PASSES', '2'))

QPERM = []
for _p in range(8):
    _a = (_p // 4) * 8 + (_p % 4)
    QPERM += [_a, _a + 4]


class Buf:
    __slots__ = ("name", "w", "r", "dsem", "dval", "psum")

    def __init__(self, name, psum=False):
        self.name = name
        self.psum = psum
        self.w = None
        self.r = {}
        self.dsem = None
        self.dval = 0


class V:
    __slots__ = ("ap", "bufs")

    def __init__(self, ap, bufs):
        self.ap = ap
        if isinstance(bufs, Buf):
            bufs = (bufs,)
        self.bufs = tuple(bufs) if bufs else ()

    def __getitem__(self, i):
        return V(self.ap[i], self.bufs)

    def re(self, pat, **kw):
        return V(self.ap.rearrange(pat, **kw), self.bufs)

    def bc(self, shape):
        return V(self.ap.to_broadcast(list(shape)), self.bufs)

    def cast(self, dt):
        return V(self.ap.bitcast(dt), self.bufs)


SAME_ENGINE_WAIT = {"pe": False, "act": True, "dve": True, "pool": True, "sp": False}


class Eng:
    def __init__(self, K, name, e):
        self.K = K
        self.name = name
        self.e = e
        self.sem = K.nc.alloc_semaphore("sem_" + name)
        self.cnt = 0
        self.seenE = {}
        self.seenD = {}

    def _wait(self, tok):
        if tok is None:
            return
        if tok[0] == "E":
            _, en, c = tok
            if self.seenE.get(en, 0) >= c:
                return
            if en == self.name and not SAME_ENGINE_WAIT[en]:
                return
            self.e.wait_ge(self.K.eng[en].sem, c)
            self.seenE[en] = c
        else:
            b = tok[1]
            if b.dsem is None:
                return
            v = b.dval
            if self.seenD.get(id(b.dsem), 0) >= v:
                return
            self.e.wait_ge(b.dsem, v)
            self.seenD[id(b.dsem)] = v

    def _deps(self, outs, ins):
        for v in ins:
            for b in v.bufs:
                self._wait(b.w)
                if b.psum:
                    for k, t in list(b.r.items()):
                        if k != self.name:
                            self._wait(t)
        for v in outs:
            for b in v.bufs:
                self._wait(b.w)
                for t in list(b.r.values()):
                    self._wait(t)

    def do(self, fn, outs, ins, inc=True):
        outs = [v for v in outs if v is not None]
        ins = [v for v in ins if v is not None]
        self._deps(outs, ins)
        inst = fn()
        tok = ("E", self.name, self.cnt + 1)
        if inc:
            self.cnt += 1
            inst.then_inc(self.sem, 1)
        for v in ins:
            for b in v.bufs:
                b.r[self.name] = tok
        for v in outs:
            for b in v.bufs:
                b.w = tok
                b.r = {}
        return inst

    def dma(self, out, in_, slot, **kw):
        self._deps([out], [in_])
        if slot.dsem is None:
            if self.K.sempool:
                slot.dsem, slot.dval = self.K.sempool.pop()
            else:
                slot.dsem = self.K.nc.alloc_semaphore("dsem_%d" % self.K.uid)
                self.K.uid += 1
                slot.dval = 0
            self.K.dma_bufs.append(slot)
        slot.dval += 16
        self.e.dma_start(out=out.ap, in_=in_.ap, **kw).then_inc(slot.dsem, 16)
        tok = ("D", slot)
        for b in in_.bufs:
            b.r[("D", id(slot))] = tok
        for b in out.bufs:
            b.w = tok
            b.r = {}


class Kern:
    def __init__(self):
        nc = bass.Bass("TRN2", target_bir_lowering=False)
        self.nc = nc
        self.eng = {}
        self.dma_bufs = []
        self.sempool = []
        for name, e in (("pe", nc.tensor), ("act", nc.scalar), ("dve", nc.vector),
                        ("pool", nc.gpsimd), ("sp", nc.sync)):
            self.eng[name] = Eng(self, name, e)
        self.pe, self.act, self.dve, self.pool, self.sp = (self.eng[n] for n in
                                                           ("pe", "act", "dve", "pool", "sp"))
        self.barsem = nc.alloc_semaphore("barsem")
        self.barcnt = 0
        self.uid = 0
        self.psum = nc.alloc_psum_tensor("psum_all", [128, 8 * 512], F32).ap()
        self.pbuf = [Buf("psb%d" % i, psum=True) for i in range(8)]

    def sb(self, stack, name, shape, dt, nbuf=None):
        def one(nm):
            h = stack.enter_context(self.nc.sbuf_tensor("sb_" + nm, list(shape), dt))
            return V(h.ap(), Buf(nm))
        if nbuf is None:
            return one(name)
        return [one("%s_%d" % (name, i)) for i in range(nbuf)]

    def ps(self, bank, nbanks=1, dt=F32):
        ap = self.psum[:, bank * 512:(bank + nbanks) * 512]
        if dt != F32:
            ap = ap.bitcast(dt)
        return V(ap, self.pbuf[bank:bank + nbanks])

    def dram(self, name, shape, dt, kind=None):
        if kind:
            return self.nc.dram_tensor(name, list(shape), dt, kind=kind).ap()
        return self.nc.dram_tensor(name, list(shape), dt).ap()

    def barrier(self):
        sp = self.sp
        for en, E in self.eng.items():
            if en != "sp" and E.cnt > 0:
                sp._wait(("E", en, E.cnt))
        for b in self.dma_bufs:
            if b.dval > 0:
                sp._wait(("D", b))
        self.barcnt += 1
        sp.e.sem_inc(self.barsem, 1)
        for en, E in self.eng.items():
            if en != "sp":
                E.e.wait_ge(self.barsem, self.barcnt)
                for en2, E2 in self.eng.items():
                    E.seenE[en2] = E2.cnt
                for b in self.dma_bufs:
                    E.seenD[id(b.dsem)] = b.dval

    def mark(self):
        return len(self.dma_bufs)

    def recycle(self, n0):
        while len(self.dma_bufs) > n0:
            b = self.dma_bufs.pop()
            self.sempool.append((b.dsem, b.dval))
            b.dsem = None

    def mm(self, out, lhsT, rhs, start=True, stop=True, inc=True):
        return self.pe.do(lambda: self.nc.tensor.matmul(out.ap, lhsT=lhsT.ap, rhs=rhs.ap,
                                                        start=start, stop=stop),
                          [out], [lhsT, rhs], inc=inc)

    def tr(self, out, in_, ident, inc=True):
        return self.pe.do(lambda: self.nc.tensor.transpose(out.ap, in_.ap, ident.ap),
                          [out], [in_, ident], inc=inc)

    def actf(self, out, in_, func, bias=None, scale=None, accum=None):
        kw = {}
        ins = [in_]
        if bias is not None:
            if isinstance(bias, V):
                kw["bias"] = bias.ap
                ins.append(bias)
            else:
                kw["bias"] = bias
        if scale is not None:
            if isinstance(scale, V):
                kw["scale"] = scale.ap
                ins.append(scale)
            else:
                kw["scale"] = scale
        outs = [out]
        if accum is not None:
            kw["accum_out"] = accum.ap
            outs.append(accum)
        return self.act.do(lambda: self.nc.scalar.activation(out=out.ap, in_=in_.ap, func=func, **kw),
                           outs, ins)

    def _veng(self, eng):
        return (self.dve, self.nc.vector) if eng == "dve" else (self.pool, self.nc.gpsimd)

    def tt(self, out, in0, in1, op, eng="dve"):
        E, e = self._veng(eng)
        return E.do(lambda: e.tensor_tensor(out=out.ap, in0=in0.ap, in1=in1.ap, op=op),
                    [out], [in0, in1])

    def ts(self, out, in0, s1, op0, s2=None, op1=None, eng="dve", accum=None):
        E, e = self._veng(eng)
        ins = [in0]
        a1 = s1
        if isinstance(s1, V):
            a1 = s1.ap
            ins.append(s1)
        a2 = s2
        if isinstance(s2, V):
            a2 = s2.ap
            ins.append(s2)
        kw = {}
        if op1 is not None:
            kw["op1"] = op1
        outs = [out]
        if accum is not None:
            kw["accum_out"] = accum.ap
            outs.append(accum)
        return E.do(lambda: e.tensor_scalar(out=out.ap, in0=in0.ap, scalar1=a1, scalar2=a2, op0=op0, **kw),
                    outs, ins)

    def stt(self, out, in0, scalar, in1, op0, op1):
        ins = [in0, in1]
        a = scalar
        if isinstance(scalar, V):
            a = scalar.ap
            ins.append(scalar)
        return self.dve.do(lambda: self.nc.vector.scalar_tensor_tensor(out=out.ap, in0=in0.ap, scalar=a,
                                                                       in1=in1.ap, op0=op0, op1=op1),
                           [out], ins)

    def cp(self, out, in_, eng="dve"):
        if eng == "act":
            return self.act.do(lambda: self.nc.scalar.copy(out=out.ap, in_=in_.ap), [out], [in_])
        E, e = self._veng(eng)
        return E.do(lambda: e.tensor_copy(out=out.ap, in_=in_.ap), [out], [in_])

    def red(self, out, in_, op=ALU.add, axis=AX.X):
        return self.dve.do(lambda: self.nc.vector.tensor_reduce(out=out.ap, in_=in_.ap, axis=axis, op=op),
                           [out], [in_])

    def recip(self, out, in_):
        return self.dve.do(lambda: self.nc.vector.reciprocal(out=out.ap, in_=in_.ap), [out], [in_])

    def memset(self, out, val, eng="dve"):
        E, e = self._veng(eng)
        return E.do(lambda: e.memset(out.ap, val), [out], [])

    def load(self, out, in_ap, q="sp", **kw):
        E = self.eng[q]
        E.dma(out, V(in_ap, ()), out.bufs[0], **kw)

    def store(self, out_ap, in_, q="sp", **kw):
        E = self.eng[q]
        E.dma(V(out_ap, ()), in_, in_.bufs[0], **kw)


def skewed(n, stages):
    k = len(stages)
    for step in range(n + k - 1):
        for si, fn in enumerate(stages):
            t = step - si
            if 0 <= t < n:
                fn(t)


def load_w_bf16(K, stack, name, dview, J, N, chunk=2048):
    w = K.sb(stack, name, [128, J, N], BF16)
    n0 = K.mark()
    with ExitStack() as st:
        stg = K.sb(st, name + "_stg", [128, chunk], F32, nbuf=3)
        i = 0
        for j in range(J):
            for c0 in range(0, N, chunk):
                c1 = min(N, c0 + chunk)
                s = stg[i % 3]
                K.load(s[:, 0:c1 - c0], dview[:, j, c0:c1])
                K.cp(w[:, j, c0:c1], s[:, 0:c1 - c0], eng=("dve", "pool", "act")[i % 3])
                i += 1
        K.barrier()
        K.recycle(n0)
    return w


def build(debug=(), phases=("attn", "ssd", "merge", "mlp")):
    K = Kern()
    nc = K.nc
    dbg = set(debug)

    def scratch(name, shape, dt):
        return K.dram(name, shape, dt, kind="ExternalOutput" if name in dbg else None)

    x_d = K.dram("x", [T, D], F32, "ExternalInput")
    c_d = K.dram("c", [128, 8], F32, "ExternalInput")
    w_ada_d = K.dram("w_ada", [D, 6 * D], F32, "ExternalInput")
    b_ada_d = K.dram("b_ada", [1, 6 * D], F32, "ExternalInput")
    n1w_d = K.dram("norm1_w", [1, D], F32, "ExternalInput")
    n2w_d = K.dram("norm2_w", [1, D], F32, "ExternalInput")
    ident_d = K.dram("ident", [128, 128], F32, "ExternalInput")
    wqkv_d = K.dram("w_qkv", [D, 1536], F32, "ExternalInput")
    wqk_d = K.dram("wqk", [1, 1280], F32, "ExternalInput")
    cs_d = K.dram("cs", [T, 64], F32, "ExternalInput")
    wao_d = K.dram("w_ao", [D, D], F32, "ExternalInput")
    wxbc_d = K.dram("w_xbc", [D, CONV], F32, "ExternalInput")
    wz_d = K.dram("w_z", [D, DI], F32, "ExternalInput")
    wdt_d = K.dram("w_dt", [D, 64], F32, "ExternalInput")
    wg_d = K.dram("w_g", [D, 2 * D], F32, "ExternalInput")
    convw_d = K.dram("convw", [128, 24, 5], F32, "ExternalInput")
    convb_d = K.dram("convb", [128, 24], F32, "ExternalInput")
    dtb_d = K.dram("dtb", [1, 64], F32, "ExternalInput")
    alog_d = K.dram("alog", [1, 64], F32, "ExternalInput")
    dfull_d = K.dram("dfull", [1, DI], F32, "ExternalInput")
    snw_d = K.dram("snw", [1, DI], F32, "ExternalInput")
    wso_d = K.dram("w_so", [DI, D], F32, "ExternalInput")
    masks_d = K.dram("masks", [128, 5, 128], F32, "ExternalInput")
    wo_d = K.dram("w_o", [D, D], F32, "ExternalInput")
    w1_d = K.dram("w_1", [D, DFF], F32, "ExternalInput")
    w2_d = K.dram("w_2", [DFF, D], F32, "ExternalInput")
    out_d = K.dram("out", [T, D], F32, "ExternalOutput")

    hT_s = scratch("hT_s", [8, 128, T + 2 * PAD], BF16)
    mod_s = scratch("mod_s", [128, 4 * D], F32)
    qT_s = scratch("qT_s", [8, 128, T], BF16)
    ao_s = scratch("ao_s", [T, D], F32)
    y_s = scratch("y_s", [T, DI], F32)
    so_s = scratch("so_s", [T, D], F32)
    x1_s = scratch("x1_s", [T, D], F32)
    h2T_s = scratch("h2T_s", [8, 128, T], BF16)
    CH = 128 * CONV

    def _flat_bf16(ap2d):
        return ap2d.bitcast(BF16).rearrange("a b -> (a b)") if ap2d.dtype != BF16 else ap2d

    x1_flat = x1_s.bitcast(BF16).rearrange("a b -> (a b)")
    qT_flat = qT_s.rearrange("j p t -> (j p t)")
    h2_flat = h2T_s.rearrange("j p t -> (j p t)")

    def xc_view(c):
        if c < 42:
            fl, i = x1_flat, c
        elif c < 63:
            fl, i = qT_flat, c - 42
        else:
            fl, i = h2_flat, c - 63
        return fl[i * CH:(i + 1) * CH].rearrange("(p f) -> p f", f=CONV)

    dtr_flat = h2T_s.bitcast(F32).rearrange("j p t -> (j p t)")

    def dtr_view(c):
        off = 2 * CH // 2 + c * 128 * 64
        return dtr_flat[off:off + 128 * 64].rearrange("(p f) -> p f", f=64)

    with ExitStack() as glob:
        identf = K.sb(glob, "identf", [128, 128], F32)
        identb = K.sb(glob, "identb", [128, 128], BF16)
        K.load(identf, ident_d)
        K.cp(identb, identf)
        n_glob = K.mark()

        with ExitStack() as ph:
            modb = K.sb(ph, "modb", [128, 6 * D], F32)
            g1 = K.sb(ph, "g1", [128, D], F32)
            n_p0 = K.mark()
            with ExitStack() as p0:
                bab = K.sb(p0, "bab", [128, 6 * D], F32)
                K.load(bab, b_ada_d.partition_broadcast(128) if False else b_ada_d.to_broadcast([128, 6 * D]))
                n1b = K.sb(p0, "n1b", [128, D], F32)
                n2b = K.sb(p0, "n2b", [128, D], F32)
                K.load(n1b, n1w_d.to_broadcast([128, D]))
                K.load(n2b, n2w_d.to_broadcast([128, D]))
                csb = K.sb(p0, "csb", [128, 8, 1], F32)
                K.load(csb, c_d.rearrange("p (j o) -> p j o", o=1))
                scs = K.sb(p0, "scs", [128, 8, 1], F32)
                K.actf(scs, csb, AF.Silu)
                lhsc = K.sb(p0, "lhsc", [128, 8, 128], F32)
                K.cp(lhsc, scs.bc([128, 8, 128]))
                wa = K.sb(p0, "wa", [128, 8, 512], F32, nbuf=2)
                w_ada_v = w_ada_d.rearrange("(j p) n -> p j n", p=128)
                for nn in range(12):
                    w = wa[nn % 2]
                    K.load(w, w_ada_v[:, :, nn * 512:(nn + 1) * 512])
                    pst = K.ps(nn % 2)
                    for j in range(8):
                        K.mm(pst, lhsc[:, j, :], w[:, j, :], start=(j == 0), stop=(j == 7), inc=(j == 7))
                    K.tt(modb[:, nn * 512:(nn + 1) * 512], pst, bab[:, nn * 512:(nn + 1) * 512], ALU.add)
                K.stt(g1, modb[:, D:2 * D], 1.0, n1b, ALU.add, ALU.mult)
                g2 = K.sb(p0, "g2", [128, D], F32)
                K.stt(g2, modb[:, 4 * D:5 * D], 1.0, n2b, ALU.add, ALU.mult)
                K.store(mod_s[:, 0:D], g2)
                K.store(mod_s[:, D:2 * D], modb[:, 3 * D:4 * D])
                K.store(mod_s[:, 2 * D:3 * D], modb[:, 2 * D:3 * D])
                K.store(mod_s[:, 3 * D:4 * D], modb[:, 5 * D:6 * D])
                K.barrier()
                K.recycle(n_p0)
            shift1 = modb[:, 0:D]

            xt = K.sb(ph, "xt", [128, D], F32, nbuf=3)
            junk = K.sb(ph, "junk", [128, D], BF16)
            t1 = K.sb(ph, "t1", [128, D], F32, nbuf=2)
            t2 = K.sb(ph, "t2", [128, D], F32, nbuf=2)
            hb = K.sb(ph, "hb", [128, D], BF16, nbuf=2)
            ss = K.sb(ph, "ss", [128, 1], F32, nbuf=2)
            rt = K.sb(ph, "rt", [128, 1], F32, nbuf=2)
            rs = K.sb(ph, "rs", [128, 1], F32, nbuf=2)
            hTs = K.sb(ph, "hTs", [128, 8, 512], BF16, nbuf=2)
            zpad = K.sb(ph, "zpad", [128, 8, PAD], BF16)
            K.memset(zpad, 0.0)
            hT_v = hT_s.rearrange("j p t -> p j t")
            K.store(hT_v[:, :, 0:PAD], zpad)
            K.store(hT_v[:, :, T + PAD:T + 2 * PAD], zpad)
            PF = 2
            for it in range(NT + PF):
                if it < NT:
                    K.load(xt[it % 3], x_d[it * 128:(it + 1) * 128, :])
                t = it - PF
                if t < 0:
                    continue
                blk, s = t // 4, t % 4
                xv = xt[t % 3]
                K.actf(junk, xv, AF.Square, accum=ss[t % 2])
                K.actf(rt[t % 2], ss[t % 2], AF.Sqrt, bias=EPS, scale=1.0 / D)
                K.recip(rs[t % 2], rt[t % 2])
                K.actf(t1[t % 2], xv, AF.Copy, scale=rs[t % 2])
                K.tt(t2[t % 2], t1[t % 2], g1, ALU.mult, eng="pool")
                K.tt(hb[t % 2], t2[t % 2], shift1, ALU.add)
                pbase = (blk % 2) * 4
                for j in range(8):
                    pv = K.ps(pbase + j // 2, dt=BF16)
                    K.tr(pv[:, (j % 2) * 512 + s * 128:(j % 2) * 512 + (s + 1) * 128],
                         hb[t % 2][:, j * 128:(j + 1) * 128], identb)
                if s == 3:
                    hts = hTs[blk % 2]
                    for b4 in range(4):
                        K.cp(hts[:, 2 * b4:2 * b4 + 2, :].re("p a t -> p (a t)"), K.ps(pbase + b4, dt=BF16),
                             eng=("dve" if b4 % 2 == 0 else "act"))
                    K.store(hT_v[:, :, PAD + blk * 512:PAD + (blk + 1) * 512], hts)
            K.barrier()
            K.recycle(n_glob)

        hT_v = hT_s.rearrange("j p t -> p j t")
        if "attn" in phases:
          with ExitStack() as pa:
            KT = K.sb(pa, "KT", [128, 2, T], BF16)
            VA = K.sb(pa, "VA", [128, NT, 4, 128], BF16)
            VA5 = VA.re("p t (s h) c -> p t s h c", h=2)
            K.memset(VA5[:, :, :, 0, 64:128], 1.0, eng="pool")
            K.memset(VA5[:, :, :, 1, 0:64], 1.0, eng="pool")
            with ExitStack() as p2:
                wqkv = load_w_bf16(K, p2, "wqkv", wqkv_d.rearrange("(j p) n -> p j n", p=128), 8, 1536)
                csl = K.sb(p2, "csl", [128, 64], F32, nbuf=3)
                wqkb = K.sb(p2, "wqkb", [128, 1280], F32)
                K.load(wqkb, wqk_d.to_broadcast([128, 1280]))
                hTl = K.sb(p2, "hTl", [128, 8, 512], BF16, nbuf=2)
                sq = K.sb(p2, "sq", [128, 20, 64], F32)
                ssq = K.sb(p2, "ssq", [128, 20, 1], F32, nbuf=2)
                rtq = K.sb(p2, "rtq", [128, 20, 1], F32, nbuf=2)
                rsq = K.sb(p2, "rsq", [128, 20, 1], F32, nbuf=2)
                qn = K.sb(p2, "qn", [128, 20, 64], F32, nbuf=2)
                qw = qn
                tA = K.sb(p2, "tA", [128, 20, 64], F32)
                tB = K.sb(p2, "tB", [128, 20, 64], F32)
                qr = K.sb(p2, "qr", [128, 20, 64], BF16, nbuf=2)
                qTst = K.sb(p2, "qTst", [128, 8, 512], BF16, nbuf=2)
                qT_v = qT_s.rearrange("j p t -> p j t")
                def qA(t):
                    blk, s = t // 4, t % 4
                    hl = hTl[blk % 2]
                    if s == 0:
                        K.load(hl, hT_v[:, :, PAD + blk * 512:PAD + (blk + 1) * 512])
                    pb = (t % 2) * 3
                    csb = csl[t % 3]
                    K.load(csb, cs_d[t * 128:(t + 1) * 128, :])
                    for n in range(3):
                        for j in range(8):
                            K.mm(K.ps(pb + n), hl[:, j, s * 128:(s + 1) * 128],
                                 wqkv[:, j, n * 512:(n + 1) * 512], start=(j == 0), stop=(j == 7), inc=(j == 7))
                    pqk = K.ps(pb, 3)[:, 0:1280].re("p (h d) -> p h d", d=64)
                    pv_ = K.ps(pb, 3)[:, 1280:1536]
                    pv4 = pv_.re("p (s h d) -> p s h d", h=2, d=64)
                    K.cp(VA5[:, t, :, 0, 0:64], pv4[:, :, 0, :], eng="act")
                    K.cp(VA5[:, t, :, 1, 64:128], pv4[:, :, 1, :], eng="act")
                    K.actf(sq, pqk, AF.Square)
                    K.red(ssq[t % 2], sq)
                    K.actf(rtq[t % 2], ssq[t % 2], AF.Sqrt, bias=EPS, scale=1.0 / DH)
                    K.recip(rsq[t % 2], rtq[t % 2])
                    K.tt(qn[t % 2], pqk, rsq[t % 2].bc([128, 20, 64]), ALU.mult)
                    K.tt(qw[t % 2], qn[t % 2], wqkb.re("p (h d) -> p h d", d=64), ALU.mult, eng="pool")
                    q4 = qw[t % 2].re("p h (a b e) -> p h a b e", a=2, b=2)
                    A4 = tA.re("p h (a b e) -> p h a b e", a=2, b=2)
                    B4 = tB.re("p h (a b e) -> p h a b e", a=2, b=2)
                    o4 = qr[t % 2].re("p h (a b e) -> p h a b e", a=2, b=2)
                    for a in range(2):
                        cosv = csb[:, a * 16:(a + 1) * 16].re("p (h b e) -> p h b e", h=1, b=1).bc([128, 20, 2, 16])
                        sinv = csb[:, 32 + a * 16:32 + (a + 1) * 16].re("p (h b e) -> p h b e", h=1, b=1).bc([128, 20, 2, 16])
                        K.tt(A4[:, :, a], q4[:, :, a], cosv, ALU.mult)
                        K.tt(B4[:, :, a], q4[:, :, a], sinv, ALU.mult, eng="pool")
                        K.tt(o4[:, :, a, 0], A4[:, :, a, 0], B4[:, :, a, 1], ALU.subtract)
                        K.tt(o4[:, :, a, 1], A4[:, :, a, 1], B4[:, :, a, 0], ALU.add, eng="pool")

                def qB(t):
                    blk, s = t // 4, t % 4
                    qst = qTst[blk % 2]
                    qrf = qr[t % 2].re("p h d -> p (h d)")
                    pq = K.ps(6, dt=BF16)
                    pk = K.ps(7, dt=BF16)
                    for p in range(8):
                        K.tr(pq[:, p * 128:(p + 1) * 128], qrf[:, p * 128:(p + 1) * 128], identb)
                    for p in range(2):
                        K.tr(pk[:, p * 128:(p + 1) * 128], qrf[:, 1024 + p * 128:1024 + (p + 1) * 128], identb)
                    K.cp(qst[:, :, s * 128:(s + 1) * 128], pq.re("p (j t) -> p j t", t=128), eng="act")
                    K.cp(KT[:, :, t * 128:(t + 1) * 128], pk[:, 0:256].re("p (j t) -> p j t", t=128))
                    if s == 3:
                        K.store(qT_v[:, :, blk * 512:(blk + 1) * 512], qst)

                skewed(NT, [qA, qB])
                K.barrier()
                K.recycle(n_glob)

            with ExitStack() as p3:
                wao = load_w_bf16(K, p3, "wao", wao_d.rearrange("(j p) n -> p j n", p=128), 8, D)
                qbd = K.sb(p3, "qbd", [128, 8, 2, 512], BF16, nbuf=2)
                for qq in qbd:
                    K.memset(qq, 0.0, eng="pool")
                PT = K.sb(p3, "PT", [128, 1024], BF16, nbuf=4)
                Osb = K.sb(p3, "Osb", [128, 512], F32, nbuf=2)
                rd = K.sb(p3, "rd", [128, 512], F32, nbuf=2)
                attnT = K.sb(p3, "attnT", [128, 8, 512], BF16, nbuf=2)
                aot = K.sb(p3, "aot", [128, D], F32, nbuf=2)
                qT_v = qT_s.rearrange("j p t -> p j t")
                NQB = T // 512

                def load_q(qb):
                    qq = qbd[qb % 2]
                    srcv = qT_v[:, :, qb * 512:(qb + 1) * 512].rearrange("p j (m t) -> p j m t", m=2)
                    for m in range(2):
                        K.load(qq[0:64, :, m, 0:256], srcv[0:64, :, m, :])
                        K.load(qq[64:128, :, m, 256:512], srcv[64:128, :, m, :])

                load_q(0)
                for qb in range(NQB):
                    if qb + 1 < NQB:
                        load_q(qb + 1)
                    ql = qbd[qb % 2]
                    aT = attnT[qb % 2]
                    NUq = 8 * NT
                    SKEW = 2
                    for n in range(NUq + SKEW):
                        if n < NUq:
                            p, kt = divmod(n, NT)
                            slot = p // 4
                            gi = qb * NUq + n
                            Sb = K.ps(2 + 2 * (gi % 3), 2)
                            for m in range(2):
                                K.mm(Sb[:, m * 512:(m + 1) * 512], KT[:, slot, kt * 128:(kt + 1) * 128],
                                     ql[:, p, m, :], inc=(m == 1))
                            K.actf(PT[gi % 4], Sb, AF.Exp, scale=DH ** -0.5)
                        if n >= SKEW:
                            n2 = n - SKEW
                            p, kt = divmod(n2, NT)
                            slot = p // 4
                            gi2 = qb * NUq + n2
                            P4 = PT[gi2 % 4].re("p (m h t) -> p m h t", m=2, h=2)
                            for half in range(2):
                                g = slot * 2 + half
                                K.mm(K.ps(half), VA[:, kt, g, :], P4[:, :, half, :],
                                     start=(kt == 0), stop=(kt == NT - 1))
                            if kt == NT - 1:
                                for half in range(2):
                                    K.cp(Osb[half], K.ps(half), eng="act")
                                    if half == 0:
                                        K.recip(rd[half][0:64, :], Osb[half][64:128, :])
                                        K.tt(aT[0:64, p, :], Osb[half][0:64, :], rd[half][0:64, :], ALU.mult)
                                    else:
                                        K.recip(rd[half][64:128, :], Osb[half][0:64, :])
                                        K.tt(aT[64:128, p, :], Osb[half][64:128, :], rd[half][64:128, :], ALU.mult)
                    for s in range(4):
                        for n in range(2):
                            for p in range(8):
                                K.mm(K.ps(6 + n), aT[:, p, s * 128:(s + 1) * 128], wao[:, p, n * 512:(n + 1) * 512],
                                     start=(p == 0), stop=(p == 7), inc=(p == 7))
                        ao = aot[s % 2]
                        K.cp(ao, K.ps(6, 2))
                        K.store(ao_s[qb * 512 + s * 128:qb * 512 + (s + 1) * 128, :], ao)
                K.barrier()
                K.recycle(n_glob)

        if "ssd" in phases:
          with ExitStack() as pss:
            wxbc = load_w_bf16(K, pss, "wxbc", wxbc_d.rearrange("(j p) n -> p j n", p=128), 8, CONV)
            wdt = load_w_bf16(K, pss, "wdt", wdt_d.rearrange("(j p) n -> p j n", p=128), 8, 64)
            msk = K.sb(pss, "msk", [128, 5, 128], F32)
            K.load(msk, masks_d)
            convw = K.sb(pss, "convw", [128, 24, 5], F32)
            K.load(convw, convw_d)
            convb = K.sb(pss, "convb", [128, 24], F32)
            K.load(convb, convb_d)
            dtb = K.sb(pss, "dtb", [128, 64], F32)
            K.load(dtb, dtb_d.to_broadcast([128, 64]))
            Ab = K.sb(pss, "Ab", [128, 64], F32)
            K.load(Ab, alog_d.to_broadcast([128, 64]))
            K.actf(Ab, Ab, AF.Exp)
            K.ts(Ab, Ab, -1.0, ALU.mult)
            Db = K.sb(pss, "Db", [128, DI], F32)
            K.load(Db, dfull_d.to_broadcast([128, DI]))
            diag = K.sb(pss, "diag", [128, 24, 5, 128], BF16)
            for ct in range(24):
                for k in range(5):
                    K.ts(diag[:, ct, k, :], identf, convw[:, ct, k:k + 1], ALU.mult)
            hw = K.sb(pss, "hw", [128, 8, 512 + 2 * PAD], BF16, nbuf=2)
            xbcT = K.sb(pss, "xbcT", [128, 24, 128 + 2 * PAD], BF16)
            xcTb = K.sb(pss, "xcT", [128, 24, 128], BF16, nbuf=2)
            dtrb = K.sb(pss, "dtrb", [128, 64], F32, nbuf=2)
            Btm = K.sb(pss, "Btm", [128, 512], BF16)
            st = K.sb(pss, "st", [128, NH, 64], F32)
            stbf = K.sb(pss, "stbf", [128, DI], BF16)
            rhi = K.sb(pss, "rhi", [128, 16, 128], BF16, nbuf=2)
            rlo = K.sb(pss, "rlo", [128, 16, 128], BF16, nbuf=2)
            mskb = K.sb(pss, "mskb", [128, 5, 128], BF16)
            K.cp(mskb, msk)
            Ebh = K.sb(pss, "Eb", [128, 16, 128], BF16, nbuf=2)
            CBm = K.sb(pss, "CBm", [128, 4, 128], BF16)
            Xb = K.sb(pss, "Xb", [128, DI], BF16)
            Xd = K.sb(pss, "Xd", [128, DI], BF16)
            ytmp = K.sb(pss, "ytmp", [128, D], F32)
            yacc = K.sb(pss, "yacc", [128, D], F32, nbuf=1) * 2
            yfin = K.sb(pss, "yfin", [128, DI], F32, nbuf=2)
            smb = [{n: K.sb(pss, "sm%d_%s" % (i, n), [128, 32], F32) for n in
                    ("x", "ab", "e", "l", "dt", "a", "acs", "expA", "dd", "dec", "cd", "w2")} for i in range(2)]
            smh = [(K.sb(pss, "ahi%d" % i, [128, 32], BF16), K.sb(pss, "alo%d" % i, [128, 32], BF16)) for i in range(2)]

            for d in range(SSD_PASSES):
                Ud = msk[:, d, :]
                Lsd = msk[:, 2 + d, :]
                ones = msk[:, 4, :]
                K.memset(st, 0.0)
                K.memset(stbf, 0.0, eng="pool")
                order = list(range(NT))[:SSD_NCH]
                if d == 1:
                    order = order[::-1]
                for ci, c in enumerate(order):
                    grp = c // 4
                    recompute = (d == 0) or not CACHE_CONV
                    if ci % 4 == 0 and recompute:
                        K.load(hw[(ci // 4) % 2], hT_v[:, :, grp * 512:grp * 512 + 512 + 2 * PAD])
                    hwv = hw[(ci // 4) % 2]
                    o = (c % 4) * 128
                    yf = yfin[ci % 2]
                    if d == 1:
                        K.load(yf, y_s[c * 128:(c + 1) * 128, :])
                    xcT = xcTb[ci % 2]
                    sm = smb[ci % 2]
                    p7 = K.ps(7)
                    dtr = dtrb[ci % 2]
                    if recompute:
                        for j in range(8):
                            K.mm(p7[:, 0:64], hwv[:, j, o + PAD:o + PAD + 128], wdt[:, j, :], start=(j == 0), stop=(j == 7), inc=(j == 7))
                        K.cp(dtr, p7[:, 0:64])
                        if CACHE_CONV:
                            K.store(dtr_view(c), dtr)
                    else:
                        if ci == 0:
                            K.load(xcTb[0].re("p a t -> p (a t)"), xc_view(order[0]))
                            K.load(dtrb[0], dtr_view(order[0]))
                        if ci + 1 < len(order):
                            K.load(xcTb[(ci + 1) % 2].re("p a t -> p (a t)"), xc_view(order[ci + 1]))
                            K.load(dtrb[(ci + 1) % 2], dtr_view(order[ci + 1]))
                    K.tt(sm["x"], dtr[:, d * 32:(d + 1) * 32], dtb[:, d * 32:(d + 1) * 32], ALU.add)
                    K.stt(sm["ab"], sm["x"], -1.0, sm["x"], ALU.mult, ALU.max)
                    K.actf(sm["e"], sm["ab"], AF.Exp, scale=-1.0)
                    K.actf(sm["l"], sm["e"], AF.Ln, bias=1.0)
                    K.stt(sm["dt"], sm["x"], 0.0, sm["l"], ALU.max, ALU.add)
                    K.tt(sm["a"], sm["dt"], Ab[:, d * 32:(d + 1) * 32], ALU.mult)
                    for w in (range(8) if recompute else ()):
                        bank = 4 + w % 3
                        for i in range(3):
                            ct = 3 * w + i
                            for j in range(8):
                                K.mm(K.ps(bank)[:, i * 132:(i + 1) * 132], wxbc[:, j, ct * 128:(ct + 1) * 128],
                                     hwv[:, j, o:o + 132], start=(j == 0), stop=(j == 7), inc=(j == 7))
                        K.cp(xbcT[:, 3 * w:3 * w + 3, :].re("p a t -> p (a t)"), K.ps(bank)[:, 0:396], eng="act")
                    K.mm(p7[:, 64:96], Ud, sm["a"])
                    K.mm(p7[:, 96:128], ones, sm["a"])
                    K.cp(sm["acs"], p7[:, 64:96])
                    K.actf(sm["expA"], p7[:, 64:96], AF.Exp)
                    K.tt(sm["dd"], p7[:, 96:128], sm["acs"], ALU.subtract)
                    K.actf(sm["dec"], sm["dd"], AF.Exp)
                    K.actf(sm["cd"], p7[:, 96:128], AF.Exp)
                    K.tt(sm["w2"], sm["dt"], sm["dec"], ALU.mult)
                    ahi, alo = smh[ci % 2]
                    K.cp(ahi, sm["a"])
                    K.tt(alo, sm["a"], ahi, ALU.subtract)
                    Ub3 = mskb[:, d, :].re("p (g l) -> p g l", g=1).bc([128, 16, 128])
                    ah3 = ahi.re("p (h o) -> p h o", o=1)
                    al3 = alo.re("p (h o) -> p h o", o=1)
                    K.tt(rlo[1], Ub3, al3[:, 16:32, :].bc([128, 16, 128]), ALU.mult, eng="pool")
                    K.tt(rhi[0], Ub3, ah3[:, 0:16, :].bc([128, 16, 128]), ALU.mult)
                    K.tt(rlo[0], Ub3, al3[:, 0:16, :].bc([128, 16, 128]), ALU.mult)
                    K.tt(rhi[1], Ub3, ah3[:, 16:32, :].bc([128, 16, 128]), ALU.mult)
                    for ct in (range(24) if recompute else ()):
                        po = K.ps(ct % 4)[:, ((ct // 4) % 4) * 128:((ct // 4) % 4 + 1) * 128]
                        for k in range(5):
                            K.mm(po, diag[:, ct, k, :], xbcT[:, ct, k:k + 128], start=(k == 0), stop=(k == 4), inc=(k == 4))
                        K.actf(xcT[:, ct, :], po, AF.Silu, bias=convb[:, ct:ct + 1])
                    if recompute and CACHE_CONV:
                        K.store(xc_view(c), xcT.re("p a t -> p (a t)"))
                    xs_ps = K.ps(4, 2, dt=BF16)
                    for ct in range(16):
                        K.tr(xs_ps[:, ct * 128:(ct + 1) * 128], xcT[:, ct, :], identb)
                    b_ps = K.ps(6, dt=BF16)
                    for g in range(4):
                        K.tr(b_ps[:, g * 128:(g + 1) * 128], xcT[:, 16 + g, :], identb)
                    K.cp(Btm, b_ps[:, 0:512])
                    xs3 = xs_ps.re("p (h e) -> p h e", e=64)
                    K.tt(Xb.re("p (h e) -> p h e", e=64), xs3, sm["dt"].re("p (h o) -> p h o", o=1).bc([128, NH, 64]), ALU.mult)
                    K.tt(Xd.re("p (h e) -> p h e", e=64), xs3, sm["w2"].re("p (h o) -> p h o", o=1).bc([128, NH, 64]), ALU.mult)
                    if d == 0:
                        K.tt(yf, xs_ps, Db, ALU.mult)
                    p3 = K.ps(3)
                    for g in range(4):
                        K.mm(p3[:, g * 128:(g + 1) * 128], xcT[:, 16 + g, :], xcT[:, 20 + g, :])
                    K.tt(CBm, p3.re("p (g l) -> p g l", l=128), Ud.re("p (g l) -> p g l", g=1).bc([128, 4, 128]), ALU.mult)
                    Lsb = mskb[:, 2 + d, :]
                    for hq in range(8):
                        pd = K.ps(hq % 2)
                        hs = slice(4 * (hq % 4), 4 * (hq % 4) + 4)
                        K.mm(pd, Lsb, rhi[hq // 4][:, hs, :].re("p h l -> p (h l)"), start=True, stop=False, inc=False)
                        K.mm(pd, Lsb, rlo[hq // 4][:, hs, :].re("p h l -> p (h l)"), start=False, stop=True)
                        K.actf(Ebh[hq // 4][:, 4 * (hq % 4):4 * (hq % 4) + 4, :].re("p h l -> p (h l)"), pd, AF.Exp)
                        if hq % 2 == 1:
                            g = hq // 2
                            mtv = Ebh[g // 2][:, 8 * (g % 2):8 * (g % 2) + 8, :]
                            K.tt(mtv, mtv, CBm[:, g:g + 1, :].bc([128, 8, 128]), ALU.mult)
                    for hh in range(2):
                        ydg = K.ps(2, 2)
                        for h in range(16 * hh, 16 * hh + 16):
                            K.mm(ydg[:, (h % 16) * 64:(h % 16 + 1) * 64], Ebh[hh][:, h % 16, :], Xb[:, h * 64:(h + 1) * 64],
                                 inc=(h % 16 == 15))
                        yof = K.ps(6, 2)
                        for g in (2 * hh, 2 * hh + 1):
                            K.mm(yof[:, (g % 2) * 512:(g % 2 + 1) * 512], xcT[:, 20 + g, :], stbf[:, g * 512:(g + 1) * 512],
                                 inc=(g % 2 == 1))
                        K.tt(ytmp.re("p (h e) -> p h e", e=64), yof.re("p (h e) -> p h e", e=64),
                             sm["expA"][:, 16 * hh:16 * hh + 16].re("p (h o) -> p h o", o=1).bc([128, 16, 64]), ALU.mult)
                        ya = yacc[hh]
                        K.tt(ya, ytmp, ydg, ALU.add)
                        K.tt(yf[:, hh * D:(hh + 1) * D], ya, yf[:, hh * D:(hh + 1) * D], ALU.add, eng="pool")
                    K.store(y_s[c * 128:(c + 1) * 128, :], yf)
                    pst = K.ps(0, 4)
                    for g in range(4):
                        K.mm(pst[:, g * 512:(g + 1) * 512], Btm[:, g * 128:(g + 1) * 128], Xd[:, g * 512:(g + 1) * 512])
                    K.tt(st, st, sm["cd"].re("p (h o) -> p h o", o=1).bc([128, NH, 64]), ALU.mult, eng="pool")
                    K.tt(st.re("p h e -> p (h e)"), st.re("p h e -> p (h e)"), pst, ALU.add)
                    K.cp(stbf, st.re("p h e -> p (h e)"), eng="pool")
                K.barrier()
            K.recycle(n_glob)

          with ExitStack() as pc:
            wz = load_w_bf16(K, pc, "wz", wz_d.rearrange("(j p) n -> p j n", p=128), 8, DI)
            wso = load_w_bf16(K, pc, "wso", wso_d.rearrange("(j p) n -> p j n", p=128), 16, D)
            snw = K.sb(pc, "snw", [128, DI], F32)
            K.load(snw, snw_d.to_broadcast([128, DI]))
            hTl = K.sb(pc, "hTl4", [128, 8, 512], BF16, nbuf=2)
            yl = K.sb(pc, "yl", [128, DI], F32, nbuf=3)
            zs = K.sb(pc, "zs", [128, DI], F32)
            yz = K.sb(pc, "yz", [128, DI], F32)
            junk2 = K.sb(pc, "junk2", [128, DI], BF16)
            yn = K.sb(pc, "yn", [128, DI], BF16, nbuf=2)
            ssdT = K.sb(pc, "ssdT", [128, 16, 128], BF16, nbuf=2)
            sot = K.sb(pc, "sot", [128, D], F32, nbuf=2)
            s1 = K.sb(pc, "s1", [128, 1], F32, nbuf=2)
            s2 = K.sb(pc, "s2", [128, 1], F32, nbuf=2)
            s3 = K.sb(pc, "s3", [128, 1], F32, nbuf=2)
            def cA(t):
                blk, s = t // 4, t % 4
                if s == 0:
                    K.load(hTl[blk % 2], hT_v[:, :, PAD + blk * 512:PAD + (blk + 1) * 512])
                hl = hTl[blk % 2]
                K.load(yl[t % 3], y_s[t * 128:(t + 1) * 128, :])
                for n in range(4):
                    for j in range(8):
                        K.mm(K.ps(n), hl[:, j, s * 128:(s + 1) * 128], wz[:, j, n * 512:(n + 1) * 512],
                             start=(j == 0), stop=(j == 7), inc=(j == 7))
                K.actf(zs, K.ps(0, 4), AF.Silu)
                K.tt(yz, yl[t % 3], zs, ALU.mult)
                K.actf(junk2, yz, AF.Square, accum=s1[t % 2])
                K.actf(s2[t % 2], s1[t % 2], AF.Sqrt, bias=EPS, scale=1.0 / DI)
                K.recip(s3[t % 2], s2[t % 2])
                K.stt(yn[t % 2], yz, s3[t % 2], snw, ALU.mult, ALU.mult)

            def cB(t):
                pT = K.ps(4, 2, dt=BF16)
                for j in range(16):
                    K.tr(pT[:, j * 128:(j + 1) * 128], yn[t % 2][:, j * 128:(j + 1) * 128], identb)
                K.cp(ssdT[t % 2].re("p j t -> p (j t)"), pT)

            def cC(t):
                for n in range(2):
                    for j in range(16):
                        K.mm(K.ps(6 + n), ssdT[t % 2][:, j, :], wso[:, j, n * 512:(n + 1) * 512],
                             start=(j == 0), stop=(j == 15), inc=(j == 15))
                K.cp(sot[t % 2], K.ps(6, 2), eng="act")
                K.store(so_s[t * 128:(t + 1) * 128, :], sot[t % 2])

            skewed(min(NT, SSD_NCH), [cA, cB, cC])
            K.barrier()
            K.recycle(n_glob)

        if "merge" in phases:
          with ExitStack() as pm:
            wg = load_w_bf16(K, pm, "wg", wg_d.rearrange("(j p) n -> p j n", p=128), 8, 2 * D)
            wo = load_w_bf16(K, pm, "wo", wo_d.rearrange("(j p) n -> p j n", p=128), 8, D)
            g2 = K.sb(pm, "g2m", [128, D], F32)
            sh2 = K.sb(pm, "sh2", [128, D], F32)
            gt1 = K.sb(pm, "gt1", [128, D], F32)
            K.load(g2, mod_s[:, 0:D])
            K.load(sh2, mod_s[:, D:2 * D])
            K.load(gt1, mod_s[:, 2 * D:3 * D])
            hTl = K.sb(pm, "hTl5", [128, 8, 512], BF16, nbuf=2)
            aol = K.sb(pm, "aol", [128, D], F32, nbuf=2)
            sol = K.sb(pm, "sol", [128, D], F32, nbuf=2)
            xl = K.sb(pm, "xl", [128, D], F32, nbuf=3)
            sig = K.sb(pm, "sig", [128, 2 * D], F32)
            m1 = K.sb(pm, "m1", [128, D], F32)
            m2 = K.sb(pm, "m2", [128, D], F32)
            mg = K.sb(pm, "mg", [128, D], BF16, nbuf=2)
            mT = K.sb(pm, "mT", [128, 8, 128], BF16, nbuf=2)
            tq = K.sb(pm, "tq", [128, D], F32)
            x1 = K.sb(pm, "x1", [128, D], F32, nbuf=2)
            junk3 = K.sb(pm, "junk3", [128, D], BF16)
            u1 = K.sb(pm, "u1", [128, D], F32)
            u2 = K.sb(pm, "u2", [128, D], F32)
            h2 = K.sb(pm, "h2", [128, D], BF16, nbuf=2)
            h2Ts = K.sb(pm, "h2Ts", [128, 8, 512], BF16, nbuf=2)
            a1 = K.sb(pm, "a1", [128, 1], F32, nbuf=2)
            a2 = K.sb(pm, "a2", [128, 1], F32, nbuf=2)
            a3 = K.sb(pm, "a3", [128, 1], F32, nbuf=2)
            h2T_v = h2T_s.rearrange("j p t -> p j t")
            def mA(t):
                blk, s = t // 4, t % 4
                if s == 0:
                    K.load(hTl[blk % 2], hT_v[:, :, PAD + blk * 512:PAD + (blk + 1) * 512])
                hl = hTl[blk % 2]
                rows = slice(t * 128, (t + 1) * 128)
                K.load(aol[t % 2], ao_s[rows, :])
                K.load(sol[t % 2], so_s[rows, :])
                K.load(xl[t % 3], x_d[rows, :])
                for n in range(4):
                    for j in range(8):
                        K.mm(K.ps(n), hl[:, j, s * 128:(s + 1) * 128], wg[:, j, n * 512:(n + 1) * 512],
                             start=(j == 0), stop=(j == 7), inc=(j == 7))
                K.actf(sig, K.ps(0, 4), AF.Sigmoid)
                K.tt(m1, sig[:, 0:D], aol[t % 2], ALU.mult)
                K.tt(m2, sig[:, D:2 * D], sol[t % 2], ALU.mult, eng="pool")
                K.tt(mg[t % 2], m1, m2, ALU.add)

            def mB(t):
                pT = K.ps(4, dt=BF16)
                for j in range(8):
                    K.tr(pT[:, j * 128:(j + 1) * 128], mg[t % 2][:, j * 128:(j + 1) * 128], identb)
                K.cp(mT[t % 2].re("p j t -> p (j t)"), pT, eng="act")

            def mC(t):
                rows = slice(t * 128, (t + 1) * 128)
                for n in range(2):
                    for j in range(8):
                        K.mm(K.ps(5 + n), mT[t % 2][:, j, :], wo[:, j, n * 512:(n + 1) * 512],
                             start=(j == 0), stop=(j == 7), inc=(j == 7))
                K.tt(tq, K.ps(5, 2), gt1, ALU.mult)
                xx = x1[t % 2]
                K.tt(xx, tq, xl[t % 3], ALU.add, eng="pool")
                K.store(x1_s[rows, :], xx)
                K.actf(junk3, xx, AF.Square, accum=a1[t % 2])
                K.actf(a2[t % 2], a1[t % 2], AF.Sqrt, bias=EPS, scale=1.0 / D)
                K.recip(a3[t % 2], a2[t % 2])
                K.actf(u1, xx, AF.Copy, scale=a3[t % 2])
                K.tt(u2, u1, g2, ALU.mult, eng="pool")
                K.tt(h2[t % 2], u2, sh2, ALU.add)

            def mD(t):
                blk, s = t // 4, t % 4
                pT2 = K.ps(7, dt=BF16)
                for j in range(8):
                    K.tr(pT2[:, j * 128:(j + 1) * 128], h2[t % 2][:, j * 128:(j + 1) * 128], identb)
                K.cp(h2Ts[blk % 2][:, :, s * 128:(s + 1) * 128], pT2.re("p (j t) -> p j t", t=128))
                if s == 3:
                    K.store(h2T_v[:, :, blk * 512:(blk + 1) * 512], h2Ts[blk % 2])

            skewed(NT, [mA, mB, mC, mD])
            K.barrier()
            K.recycle(n_glob)

        if "mlp" in phases:
          with ExitStack() as pf:
            w1 = load_w_bf16(K, pf, "w1", w1_d.rearrange("(j p) n -> p j n", p=128), 8, DFF)
            w2 = load_w_bf16(K, pf, "w2", w2_d.rearrange("(j p) n -> p j n", p=128), 32, D)
            gt2 = K.sb(pf, "gt2", [128, D], F32)
            K.load(gt2, mod_s[:, 3 * D:4 * D])
            TB = 256
            h2l = K.sb(pf, "h2l", [128, 8, TB], BF16, nbuf=2)
            uT = K.sb(pf, "uT", [128, 32, TB], BF16)
            rl = K.sb(pf, "rl", [128, TB], F32, nbuf=2)
            x1l = K.sb(pf, "x1l", [128, D], F32, nbuf=2)
            tf = K.sb(pf, "tf", [128, D], F32)
            ot = K.sb(pf, "ot", [128, D], F32, nbuf=2)
            h2T_v = h2T_s.rearrange("j p t -> p j t")
            NB = T // TB
            K.load(h2l[0], h2T_v[:, :, 0:TB])
            for blk in range(NB):
                if blk + 1 < NB:
                    K.load(h2l[(blk + 1) % 2], h2T_v[:, :, (blk + 1) * TB:(blk + 2) * TB])
                hh = h2l[blk % 2]
                for fc in range(32):
                    pf_ = K.ps(fc % 2)[:, 0:TB]
                    for j in range(8):
                        K.mm(pf_, w1[:, j, fc * 128:(fc + 1) * 128], hh[:, j, :], start=(j == 0), stop=(j == 7), inc=(j == 7))
                    K.actf(rl[fc % 2], pf_, AF.Relu)
                    K.tt(uT[:, fc, :], rl[fc % 2], rl[fc % 2], ALU.mult, eng=("dve" if fc % 2 == 0 else "pool"))
                for s in range(TB // 128):
                    tt_ = blk * (TB // 128) + s
                    rows = slice(tt_ * 128, (tt_ + 1) * 128)
                    K.load(x1l[tt_ % 2], x1_s[rows, :])
                    pb = 2 + 2 * (tt_ % 2)
                    for n in range(2):
                        for fc in range(32):
                            K.mm(K.ps(pb + n), uT[:, fc, s * 128:(s + 1) * 128], w2[:, fc, n * 512:(n + 1) * 512],
                                 start=(fc == 0), stop=(fc == 31), inc=(fc == 31))
                    K.tt(tf, K.ps(pb, 2), gt2, ALU.mult)
                    K.tt(ot[tt_ % 2], tf, x1l[tt_ % 2], ALU.add, eng="pool")
                    K.store(out_d[rows, :], ot[tt_ % 2])
            K.barrier()

        K.barrier()
    return K


def _bf16(a):
    return np.asarray(a, dtype=np.float32).astype(ml_dtypes.bfloat16)


def make_inputs(inputs, b):
    m = {}
    m["x"] = np.ascontiguousarray(inputs["x"][b])
    m["c"] = np.ascontiguousarray(np.asarray(inputs["c"][b]).reshape(8, 128).T)
    m["w_ada"] = np.ascontiguousarray(inputs["w_ada"][0])
    m["b_ada"] = np.ascontiguousarray(inputs["b_ada"][0].reshape(1, -1))
    m["norm1_w"] = np.ascontiguousarray(inputs["norm1_w"][0].reshape(1, -1))
    m["norm2_w"] = np.ascontiguousarray(inputs["norm2_w"][0].reshape(1, -1))
    m["ident"] = np.eye(128, dtype=np.float32)
    w_in = inputs["w_in"][0]
    wq = w_in[:, 0:1024].reshape(D, 16, 64)[:, QPERM, :].reshape(D, 1024)
    m["w_qkv"] = np.ascontiguousarray(np.concatenate([wq, w_in[:, 1024:1536]], axis=1))
    m["wqk"] = np.ascontiguousarray(np.concatenate([np.tile(inputs["q_norm_w"][0], 16),
                                                      np.tile(inputs["k_norm_w"][0], 4)]).reshape(1, 1280))
    m["cs"] = rope_table()
    m["w_ao"] = np.ascontiguousarray(inputs["w_attn_out"][0].reshape(16, 64, D)[QPERM].reshape(D, D))
    m["w_xbc"] = np.ascontiguousarray(w_in[:, 1536:4608])
    m["w_z"] = np.ascontiguousarray(w_in[:, 4608:6656])
    m["w_dt"] = np.ascontiguousarray(w_in[:, 6656:6720])
    m["w_g"] = np.ascontiguousarray(w_in[:, 6720:8768])
    m["convw"] = np.ascontiguousarray(inputs["conv_w"][0].reshape(5, 24, 128).transpose(2, 1, 0))
    m["convb"] = np.ascontiguousarray(inputs["conv_b"][0].reshape(24, 128).T)
    m["dtb"] = np.ascontiguousarray(inputs["dt_bias"][0].reshape(1, 64))
    m["alog"] = np.ascontiguousarray(inputs["A_log"][0].reshape(1, 64))
    m["dfull"] = np.ascontiguousarray(np.repeat(inputs["ssd_D"][0], 64).reshape(1, DI))
    m["snw"] = np.ascontiguousarray(inputs["ssd_norm_w"][0].reshape(1, DI))
    m["w_so"] = np.ascontiguousarray(inputs["w_ssd_out"][0])
    i = np.arange(128)
    uf = (i[:, None] <= i[None, :]).astype(np.float32)
    lf = (i[:, None] > i[None, :]).astype(np.float32)
    m["masks"] = np.ascontiguousarray(np.stack([uf, uf.T, lf, lf.T, np.ones((128, 128), np.float32)], axis=1))
    m["w_o"] = np.ascontiguousarray(inputs["w_o"][0])
    m["w_1"] = np.ascontiguousarray(inputs["w_mlp1"][0])
    m["w_2"] = np.ascontiguousarray(inputs["w_mlp2"][0])
    return m


def rope_table():
    pos = np.arange(T)
    pr = (pos // 64).astype(np.float32)
    pc = (pos % 64).astype(np.float32)
    inv = (np.float32(10000.0) ** (-np.arange(0, 32, 2, dtype=np.float32) / np.float32(32))).astype(np.float32)
    ar = pr[:, None] * inv[None, :]
    ac = pc[:, None] * inv[None, :]
    return np.ascontiguousarray(np.concatenate([np.cos(ar), np.cos(ac), np.sin(ar), np.sin(ac)], axis=1).astype(np.float32))


def kernel(**inputs):
    inputs = {k: np.asarray(v) for k, v in inputs.items()}
    K = build()
    in_maps = [make_inputs(inputs, b) for b in range(8)]
    res = run_bass_kernel_spmd(K.nc, in_maps, core_ids=list(range(8)))
    return np.stack([np.asarray(r["out"]) for r in res.results], axis=0).astype(np.float32)
```

```python
import numpy as np
import ml_dtypes
from contextlib import ExitStack
import concourse.bass as bass
import concourse.mybir as mybir
from concourse.bass_utils import run_bass_kernel_spmd

F32, BF16 = mybir.dt.float32, mybir.dt.bfloat16
AF = mybir.ActivationFunctionType
ALU = mybir.AluOpType
AX = mybir.AxisListType

D = 1024
T = 8192
NT = T // 128
EPS = 1e-6
NQH, NKV, DH = 16, 4, 64
DI = 2048
NH = 32
NG = 4
DS = 128
CONV = 3072
DFF = 4096
import os
PAD = 2
N_WARM = int(os.environ.get('N_WARM', '1'))
DBG_SKIP = int(os.environ.get('DBG_SKIP', '0'))
CACHE_CONV = bool(int(os.environ.get('CACHE_CONV', '1')))
SSD_NCH = int(os.environ.get('SSD_NCH', '64'))
SSD_PASSES = int(os.environ.get('SSD_PASSES', '2'))

QPERM = []
for _p in range(8):
    _a = (_p // 4) * 8 + (_p % 4)
    QPERM += [_a, _a + 4]


class Buf:
    __slots__ = ("name", "w", "r", "dsem", "dval", "psum")

    def __init__(self, name, psum=False):
        self.name = name
        self.psum = psum
        self.w = None
        self.r = {}
        self.dsem = None
        self.dval = 0


class V:
    __slots__ = ("ap", "bufs")

    def __init__(self, ap, bufs):
        self.ap = ap
        if isinstance(bufs, Buf):
            bufs = (bufs,)
        self.bufs = tuple(bufs) if bufs else ()

    def __getitem__(self, i):
        return V(self.ap[i], self.bufs)

    def re(self, pat, **kw):
        return V(self.ap.rearrange(pat, **kw), self.bufs)

    def bc(self, shape):
        return V(self.ap.to_broadcast(list(shape)), self.bufs)

    def cast(self, dt):
        return V(self.ap.bitcast(dt), self.bufs)


SAME_ENGINE_WAIT = {"pe": False, "act": True, "dve": True, "pool": True, "sp": False}


class Eng:
    def __init__(self, K, name, e):
        self.K = K
        self.name = name
        self.e = e
        self.sem = K.nc.alloc_semaphore("sem_" + name)
        self.cnt = 0
        self.seenE = {}
        self.seenD = {}

    def _wait(self, tok):
        if tok is None:
            return
        if tok[0] == "E":
            _, en, c = tok
            if self.seenE.get(en, 0) >= c:
                return
            if en == self.name and not SAME_ENGINE_WAIT[en]:
                return
            self.e.wait_ge(self.K.eng[en].sem, c)
            self.seenE[en] = c
        else:
            b = tok[1]
            if b.dsem is None:
                return
            v = b.dval
            if self.seenD.get(id(b.dsem), 0) >= v:
                return
            self.e.wait_ge(b.dsem, v)
            self.seenD[id(b.dsem)] = v

    def _deps(self, outs, ins):
        for v in ins:
            for b in v.bufs:
                self._wait(b.w)
                if b.psum:
                    for k, t in list(b.r.items()):
                        if k != self.name:
                            self._wait(t)
        for v in outs:
            for b in v.bufs:
                self._wait(b.w)
                for t in list(b.r.values()):
                    self._wait(t)

    def do(self, fn, outs, ins, inc=True):
        outs = [v for v in outs if v is not None]
        ins = [v for v in ins if v is not None]
        self._deps(outs, ins)
        inst = fn()
        tok = ("E", self.name, self.cnt + 1)
        if inc:
            self.cnt += 1
            inst.then_inc(self.sem, 1)
        for v in ins:
            for b in v.bufs:
                b.r[self.name] = tok
        for v in outs:
            for b in v.bufs:
                b.w = tok
                b.r = {}
        return inst

    def dma(self, out, in_, slot, **kw):
        self._deps([out], [in_])
        if slot.dsem is None:
            if self.K.sempool:
                slot.dsem, slot.dval = self.K.sempool.pop()
            else:
                slot.dsem = self.K.nc.alloc_semaphore("dsem_%d" % self.K.uid)
                self.K.uid += 1
                slot.dval = 0
            self.K.dma_bufs.append(slot)
        slot.dval += 16
        self.e.dma_start(out=out.ap, in_=in_.ap, **kw).then_inc(slot.dsem, 16)
        tok = ("D", slot)
        for b in in_.bufs:
            b.r[("D", id(slot))] = tok
        for b in out.bufs:
            b.w = tok
            b.r = {}


class Kern:
    def __init__(self):
        nc = bass.Bass("TRN2", target_bir_lowering=False)
        self.nc = nc
        self.eng = {}
        self.dma_bufs = []
        self.sempool = []
        for name, e in (("pe", nc.tensor), ("act", nc.scalar), ("dve", nc.vector),
                        ("pool", nc.gpsimd), ("sp", nc.sync)):
            self.eng[name] = Eng(self, name, e)
        self.pe, self.act, self.dve, self.pool, self.sp = (self.eng[n] for n in
                                                           ("pe", "act", "dve", "pool", "sp"))
        self.barsem = nc.alloc_semaphore("barsem")
        self.barcnt = 0
        self.uid = 0
        self.psum = nc.alloc_psum_tensor("psum_all", [128, 8 * 512], F32).ap()
        self.pbuf = [Buf("psb%d" % i, psum=True) for i in range(8)]

    def sb(self, stack, name, shape, dt, nbuf=None):
        def one(nm):
            h = stack.enter_context(self.nc.sbuf_tensor("sb_" + nm, list(shape), dt))
            return V(h.ap(), Buf(nm))
        if nbuf is None:
            return one(name)
        return [one("%s_%d" % (name, i)) for i in range(nbuf)]

    def ps(self, bank, nbanks=1, dt=F32):
        ap = self.psum[:, bank * 512:(bank + nbanks) * 512]
        if dt != F32:
            ap = ap.bitcast(dt)
        return V(ap, self.pbuf[bank:bank + nbanks])

    def dram(self, name, shape, dt, kind=None):
        if kind:
            return self.nc.dram_tensor(name, list(shape), dt, kind=kind).ap()
        return self.nc.dram_tensor(name, list(shape), dt).ap()

    def barrier(self):
        sp = self.sp
        for en, E in self.eng.items():
            if en != "sp" and E.cnt > 0:
                sp._wait(("E", en, E.cnt))
        for b in self.dma_bufs:
            if b.dval > 0:
                sp._wait(("D", b))
        self.barcnt += 1
        sp.e.sem_inc(self.barsem, 1)
        for en, E in self.eng.items():
            if en != "sp":
                E.e.wait_ge(self.barsem, self.barcnt)
                for en2, E2 in self.eng.items():
                    E.seenE[en2] = E2.cnt
                for b in self.dma_bufs:
                    E.seenD[id(b.dsem)] = b.dval

    def mark(self):
        return len(self.dma_bufs)

    def recycle(self, n0):
        while len(self.dma_bufs) > n0:
            b = self.dma_bufs.pop()
            self.sempool.append((b.dsem, b.dval))
            b.dsem = None

    def mm(self, out, lhsT, rhs, start=True, stop=True, inc=True):
        return self.pe.do(lambda: self.nc.tensor.matmul(out.ap, lhsT=lhsT.ap, rhs=rhs.ap,
                                                        start=start, stop=stop),
                          [out], [lhsT, rhs], inc=inc)

    def tr(self, out, in_, ident, inc=True):
        return self.pe.do(lambda: self.nc.tensor.transpose(out.ap, in_.ap, ident.ap),
                          [out], [in_, ident], inc=inc)

    def actf(self, out, in_, func, bias=None, scale=None, accum=None):
        kw = {}
        ins = [in_]
        if bias is not None:
            if isinstance(bias, V):
                kw["bias"] = bias.ap
                ins.append(bias)
            else:
                kw["bias"] = bias
        if scale is not None:
            if isinstance(scale, V):
                kw["scale"] = scale.ap
                ins.append(scale)
            else:
                kw["scale"] = scale
        outs = [out]
        if accum is not None:
            kw["accum_out"] = accum.ap
            outs.append(accum)
        return self.act.do(lambda: self.nc.scalar.activation(out=out.ap, in_=in_.ap, func=func, **kw),
                           outs, ins)

    def _veng(self, eng):
        return (self.dve, self.nc.vector) if eng == "dve" else (self.pool, self.nc.gpsimd)

    def tt(self, out, in0, in1, op, eng="dve"):
        E, e = self._veng(eng)
        return E.do(lambda: e.tensor_tensor(out=out.ap, in0=in0.ap, in1=in1.ap, op=op),
                    [out], [in0, in1])

    def ts(self, out, in0, s1, op0, s2=None, op1=None, eng="dve", accum=None):
        E, e = self._veng(eng)
        ins = [in0]
        a1 = s1
        if isinstance(s1, V):
            a1 = s1.ap
            ins.append(s1)
        a2 = s2
        if isinstance(s2, V):
            a2 = s2.ap
            ins.append(s2)
        kw = {}
        if op1 is not None:
            kw["op1"] = op1
        outs = [out]
        if accum is not None:
            kw["accum_out"] = accum.ap
            outs.append(accum)
        return E.do(lambda: e.tensor_scalar(out=out.ap, in0=in0.ap, scalar1=a1, scalar2=a2, op0=op0, **kw),
                    outs, ins)

    def stt(self, out, in0, scalar, in1, op0, op1):
        ins = [in0, in1]
        a = scalar
        if isinstance(scalar, V):
            a = scalar.ap
            ins.append(scalar)
        return self.dve.do(lambda: self.nc.vector.scalar_tensor_tensor(out=out.ap, in0=in0.ap, scalar=a,
                                                                       in1=in1.ap, op0=op0, op1=op1),
                           [out], ins)

    def cp(self, out, in_, eng="dve"):
        if eng == "act":
            return self.act.do(lambda: self.nc.scalar.copy(out=out.ap, in_=in_.ap), [out], [in_])
        E, e = self._veng(eng)
        return E.do(lambda: e.tensor_copy(out=out.ap, in_=in_.ap), [out], [in_])

    def red(self, out, in_, op=ALU.add, axis=AX.X):
        return self.dve.do(lambda: self.nc.vector.tensor_reduce(out=out.ap, in_=in_.ap, axis=axis, op=op),
                           [out], [in_])

    def recip(self, out, in_):
        return self.dve.do(lambda: self.nc.vector.reciprocal(out=out.ap, in_=in_.ap), [out], [in_])

    def memset(self, out, val, eng="dve"):
        E, e = self._veng(eng)
        return E.do(lambda: e.memset(out.ap, val), [out], [])

    def load(self, out, in_ap, q="sp", **kw):
        E = self.eng[q]
        E.dma(out, V(in_ap, ()), out.bufs[0], **kw)

    def store(self, out_ap, in_, q="sp", **kw):
        E = self.eng[q]
        E.dma(V(out_ap, ()), in_, in_.bufs[0], **kw)


def skewed(n, stages):
    k = len(stages)
    for step in range(n + k - 1):
        for si, fn in enumerate(stages):
            t = step - si
            if 0 <= t < n:
                fn(t)


def load_w_bf16(K, stack, name, dview, J, N, chunk=2048):
    w = K.sb(stack, name, [128, J, N], BF16)
    n0 = K.mark()
    with ExitStack() as st:
        stg = K.sb(st, name + "_stg", [128, chunk], F32, nbuf=3)
        i = 0
        for j in range(J):
            for c0 in range(0, N, chunk):
                c1 = min(N, c0 + chunk)
                s = stg[i % 3]
                K.load(s[:, 0:c1 - c0], dview[:, j, c0:c1])
                K.cp(w[:, j, c0:c1], s[:, 0:c1 - c0], eng=("dve", "pool", "act")[i % 3])
                i += 1
        K.barrier()
        K.recycle(n0)
    return w


def build(debug=(), phases=("attn", "ssd", "merge", "mlp")):
    K = Kern()
    nc = K.nc
    dbg = set(debug)

    def scratch(name, shape, dt):
        return K.dram(name, shape, dt, kind="ExternalOutput" if name in dbg else None)

    x_d = K.dram("x", [T, D], F32, "ExternalInput")
    c_d = K.dram("c", [128, 8], F32, "ExternalInput")
    w_ada_d = K.dram("w_ada", [D, 6 * D], F32, "ExternalInput")
    b_ada_d = K.dram("b_ada", [1, 6 * D], F32, "ExternalInput")
    n1w_d = K.dram("norm1_w", [1, D], F32, "ExternalInput")
    n2w_d = K.dram("norm2_w", [1, D], F32, "ExternalInput")
    ident_d = K.dram("ident", [128, 128], F32, "ExternalInput")
    wqkv_d = K.dram("w_qkv", [D, 1536], F32, "ExternalInput")
    wqk_d = K.dram("wqk", [1, 1280], F32, "ExternalInput")
    cs_d = K.dram("cs", [T, 64], F32, "ExternalInput")
    wao_d = K.dram("w_ao", [D, D], F32, "ExternalInput")
    wxbc_d = K.dram("w_xbc", [D, CONV], F32, "ExternalInput")
    wz_d = K.dram("w_z", [D, DI], F32, "ExternalInput")
    wdt_d = K.dram("w_dt", [D, 64], F32, "ExternalInput")
    wg_d = K.dram("w_g", [D, 2 * D], F32, "ExternalInput")
    convw_d = K.dram("convw", [128, 24, 5], F32, "ExternalInput")
    convb_d = K.dram("convb", [128, 24], F32, "ExternalInput")
    dtb_d = K.dram("dtb", [1, 64], F32, "ExternalInput")
    alog_d = K.dram("alog", [1, 64], F32, "ExternalInput")
    dfull_d = K.dram("dfull", [1, DI], F32, "ExternalInput")
    snw_d = K.dram("snw", [1, DI], F32, "ExternalInput")
    wso_d = K.dram("w_so", [DI, D], F32, "ExternalInput")
    masks_d = K.dram("masks", [128, 5, 128], F32, "ExternalInput")
    wo_d = K.dram("w_o", [D, D], F32, "ExternalInput")
    w1_d = K.dram("w_1", [D, DFF], F32, "ExternalInput")
    w2_d = K.dram("w_2", [DFF, D], F32, "ExternalInput")
    out_d = K.dram("out", [T, D], F32, "ExternalOutput")

    hT_s = scratch("hT_s", [8, 128, T + 2 * PAD], BF16)
    mod_s = scratch("mod_s", [128, 4 * D], F32)
    qT_s = scratch("qT_s", [8, 128, T], BF16)
    ao_s = scratch("ao_s", [T, D], F32)
    y_s = scratch("y_s", [T, DI], F32)
    so_s = scratch("so_s", [T, D], F32)
    x1_s = scratch("x1_s", [T, D], F32)
    h2T_s = scratch("h2T_s", [8, 128, T], BF16)
    CH = 128 * CONV

    def _flat_bf16(ap2d):
        return ap2d.bitcast(BF16).rearrange("a b -> (a b)") if ap2d.dtype != BF16 else ap2d

    x1_flat = x1_s.bitcast(BF16).rearrange("a b -> (a b)")
    qT_flat = qT_s.rearrange("j p t -> (j p t)")
    h2_flat = h2T_s.rearrange("j p t -> (j p t)")

    def xc_view(c):
        if c < 42:
            fl, i = x1_flat, c
        elif c < 63:
            fl, i = qT_flat, c - 42
        else:
            fl, i = h2_flat, c - 63
        return fl[i * CH:(i + 1) * CH].rearrange("(p f) -> p f", f=CONV)

    dtr_flat = h2T_s.bitcast(F32).rearrange("j p t -> (j p t)")

    def dtr_view(c):
        off = 2 * CH // 2 + c * 128 * 64
        return dtr_flat[off:off + 128 * 64].rearrange("(p f) -> p f", f=64)

    with ExitStack() as glob:
        identf = K.sb(glob, "identf", [128, 128], F32)
        identb = K.sb(glob, "identb", [128, 128], BF16)
        K.load(identf, ident_d)
        K.cp(identb, identf)
        n_glob = K.mark()

        with ExitStack() as ph:
            modb = K.sb(ph, "modb", [128, 6 * D], F32)
            g1 = K.sb(ph, "g1", [128, D], F32)
            n_p0 = K.mark()
            with ExitStack() as p0:
                bab = K.sb(p0, "bab", [128, 6 * D], F32)
                K.load(bab, b_ada_d.partition_broadcast(128) if False else b_ada_d.to_broadcast([128, 6 * D]))
                n1b = K.sb(p0, "n1b", [128, D], F32)
                n2b = K.sb(p0, "n2b", [128, D], F32)
                K.load(n1b, n1w_d.to_broadcast([128, D]))
                K.load(n2b, n2w_d.to_broadcast([128, D]))
                csb = K.sb(p0, "csb", [128, 8, 1], F32)
                K.load(csb, c_d.rearrange("p (j o) -> p j o", o=1))
                scs = K.sb(p0, "scs", [128, 8, 1], F32)
                K.actf(scs, csb, AF.Silu)
                lhsc = K.sb(p0, "lhsc", [128, 8, 128], F32)
                K.cp(lhsc, scs.bc([128, 8, 128]))
                wa = K.sb(p0, "wa", [128, 8, 512], F32, nbuf=2)
                w_ada_v = w_ada_d.rearrange("(j p) n -> p j n", p=128)
                for nn in range(12):
                    w = wa[nn % 2]
                    K.load(w, w_ada_v[:, :, nn * 512:(nn + 1) * 512])
                    pst = K.ps(nn % 2)
                    for j in range(8):
                        K.mm(pst, lhsc[:, j, :], w[:, j, :], start=(j == 0), stop=(j == 7), inc=(j == 7))
                    K.tt(modb[:, nn * 512:(nn + 1) * 512], pst, bab[:, nn * 512:(nn + 1) * 512], ALU.add)
                K.stt(g1, modb[:, D:2 * D], 1.0, n1b, ALU.add, ALU.mult)
                g2 = K.sb(p0, "g2", [128, D], F32)
                K.stt(g2, modb[:, 4 * D:5 * D], 1.0, n2b, ALU.add, ALU.mult)
                K.store(mod_s[:, 0:D], g2)
                K.store(mod_s[:, D:2 * D], modb[:, 3 * D:4 * D])
                K.store(mod_s[:, 2 * D:3 * D], modb[:, 2 * D:3 * D])
                K.store(mod_s[:, 3 * D:4 * D], modb[:, 5 * D:6 * D])
                K.barrier()
                K.recycle(n_p0)
            shift1 = modb[:, 0:D]

            xt = K.sb(ph, "xt", [128, D], F32, nbuf=3)
            junk = K.sb(ph, "junk", [128, D], BF16)
            t1 = K.sb(ph, "t1", [128, D], F32, nbuf=2)
            t2 = K.sb(ph, "t2", [128, D], F32, nbuf=2)
            hb = K.sb(ph, "hb", [128, D], BF16, nbuf=2)
            ss = K.sb(ph, "ss", [128, 1], F32, nbuf=2)
            rt = K.sb(ph, "rt", [128, 1], F32, nbuf=2)
            rs = K.sb(ph, "rs", [128, 1], F32, nbuf=2)
            hTs = K.sb(ph, "hTs", [128, 8, 512], BF16, nbuf=2)
            zpad = K.sb(ph, "zpad", [128, 8, PAD], BF16)
            K.memset(zpad, 0.0)
            hT_v = hT_s.rearrange("j p t -> p j t")
            K.store(hT_v[:, :, 0:PAD], zpad)
            K.store(hT_v[:, :, T + PAD:T + 2 * PAD], zpad)
            PF = 2
            for it in range(NT + PF):
                if it < NT:
                    K.load(xt[it % 3], x_d[it * 128:(it + 1) * 128, :])
                t = it - PF
                if t < 0:
                    continue
                blk, s = t // 4, t % 4
                xv = xt[t % 3]
                K.actf(junk, xv, AF.Square, accum=ss[t % 2])
                K.actf(rt[t % 2], ss[t % 2], AF.Sqrt, bias=EPS, scale=1.0 / D)
                K.recip(rs[t % 2], rt[t % 2])
                K.actf(t1[t % 2], xv, AF.Copy, scale=rs[t % 2])
                K.tt(t2[t % 2], t1[t % 2], g1, ALU.mult, eng="pool")
                K.tt(hb[t % 2], t2[t % 2], shift1, ALU.add)
                pbase = (blk % 2) * 4
                for j in range(8):
                    pv = K.ps(pbase + j // 2, dt=BF16)
                    K.tr(pv[:, (j % 2) * 512 + s * 128:(j % 2) * 512 + (s + 1) * 128],
                         hb[t % 2][:, j * 128:(j + 1) * 128], identb)
                if s == 3:
                    hts = hTs[blk % 2]
                    for b4 in range(4):
                        K.cp(hts[:, 2 * b4:2 * b4 + 2, :].re("p a t -> p (a t)"), K.ps(pbase + b4, dt=BF16),
                             eng=("dve" if b4 % 2 == 0 else "act"))
                    K.store(hT_v[:, :, PAD + blk * 512:PAD + (blk + 1) * 512], hts)
            K.barrier()
            K.recycle(n_glob)

        hT_v = hT_s.rearrange("j p t -> p j t")
        if "attn" in phases:
          with ExitStack() as pa:
            KT = K.sb(pa, "KT", [128, 2, T], BF16)
            VA = K.sb(pa, "VA", [128, NT, 4, 128], BF16)
            VA5 = VA.re("p t (s h) c -> p t s h c", h=2)
            K.memset(VA5[:, :, :, 0, 64:128], 1.0, eng="pool")
            K.memset(VA5[:, :, :, 1, 0:64], 1.0, eng="pool")
            with ExitStack() as p2:
                wqkv = load_w_bf16(K, p2, "wqkv", wqkv_d.rearrange("(j p) n -> p j n", p=128), 8, 1536)
                csl = K.sb(p2, "csl", [128, 64], F32, nbuf=3)
                wqkb = K.sb(p2, "wqkb", [128, 1280], F32)
                K.load(wqkb, wqk_d.to_broadcast([128, 1280]))
                hTl = K.sb(p2, "hTl", [128, 8, 512], BF16, nbuf=2)
                sq = K.sb(p2, "sq", [128, 20, 64], F32)
                ssq = K.sb(p2, "ssq", [128, 20, 1], F32, nbuf=2)
                rtq = K.sb(p2, "rtq", [128, 20, 1], F32, nbuf=2)
                rsq = K.sb(p2, "rsq", [128, 20, 1], F32, nbuf=2)
                qn = K.sb(p2, "qn", [128, 20, 64], F32, nbuf=2)
                qw = qn
                tA = K.sb(p2, "tA", [128, 20, 64], F32)
                tB = K.sb(p2, "tB", [128, 20, 64], F32)
                qr = K.sb(p2, "qr", [128, 20, 64], BF16, nbuf=2)
                qTst = K.sb(p2, "qTst", [128, 8, 512], BF16, nbuf=2)
                qT_v = qT_s.rearrange("j p t -> p j t")
                def qA(t):
                    blk, s = t // 4, t % 4
                    hl = hTl[blk % 2]
                    if s == 0:
                        K.load(hl, hT_v[:, :, PAD + blk * 512:PAD + (blk + 1) * 512])
                    pb = (t % 2) * 3
                    csb = csl[t % 3]
                    K.load(csb, cs_d[t * 128:(t + 1) * 128, :])
                    for n in range(3):
                        for j in range(8):
                            K.mm(K.ps(pb + n), hl[:, j, s * 128:(s + 1) * 128],
                                 wqkv[:, j, n * 512:(n + 1) * 512], start=(j == 0), stop=(j == 7), inc=(j == 7))
                    pqk = K.ps(pb, 3)[:, 0:1280].re("p (h d) -> p h d", d=64)
                    pv_ = K.ps(pb, 3)[:, 1280:1536]
                    pv4 = pv_.re("p (s h d) -> p s h d", h=2, d=64)
                    K.cp(VA5[:, t, :, 0, 0:64], pv4[:, :, 0, :], eng="act")
                    K.cp(VA5[:, t, :, 1, 64:128], pv4[:, :, 1, :], eng="act")
                    K.actf(sq, pqk, AF.Square)
                    K.red(ssq[t % 2], sq)
                    K.actf(rtq[t % 2], ssq[t % 2], AF.Sqrt, bias=EPS, scale=1.0 / DH)
                    K.recip(rsq[t % 2], rtq[t % 2])
                    K.tt(qn[t % 2], pqk, rsq[t % 2].bc([128, 20, 64]), ALU.mult)
                    K.tt(qw[t % 2], qn[t % 2], wqkb.re("p (h d) -> p h d", d=64), ALU.mult, eng="pool")
                    q4 = qw[t % 2].re("p h (a b e) -> p h a b e", a=2, b=2)
                    A4 = tA.re("p h (a b e) -> p h a b e", a=2, b=2)
                    B4 = tB.re("p h (a b e) -> p h a b e", a=2, b=2)
                    o4 = qr[t % 2].re("p h (a b e) -> p h a b e", a=2, b=2)
                    for a in range(2):
                        cosv = csb[:, a * 16:(a + 1) * 16].re("p (h b e) -> p h b e", h=1, b=1).bc([128, 20, 2, 16])
                        sinv = csb[:, 32 + a * 16:32 + (a + 1) * 16].re("p (h b e) -> p h b e", h=1, b=1).bc([128, 20, 2, 16])
                        K.tt(A4[:, :, a], q4[:, :, a], cosv, ALU.mult)
                        K.tt(B4[:, :, a], q4[:, :, a], sinv, ALU.mult, eng="pool")
                        K.tt(o4[:, :, a, 0], A4[:, :, a, 0], B4[:, :, a, 1], ALU.subtract)
                        K.tt(o4[:, :, a, 1], A4[:, :, a, 1], B4[:, :, a, 0], ALU.add, eng="pool")

                def qB(t):
                    blk, s = t // 4, t % 4
                    qst = qTst[blk % 2]
                    qrf = qr[t % 2].re("p h d -> p (h d)")
                    pq = K.ps(6, dt=BF16)
                    pk = K.ps(7, dt=BF16)
                    for p in range(8):
                        K.tr(pq[:, p * 128:(p + 1) * 128], qrf[:, p * 128:(p + 1) * 128], identb)
                    for p in range(2):
                        K.tr(pk[:, p * 128:(p + 1) * 128], qrf[:, 1024 + p * 128:1024 + (p + 1) * 128], identb)
                    K.cp(qst[:, :, s * 128:(s + 1) * 128], pq.re("p (j t) -> p j t", t=128), eng="act")
                    K.cp(KT[:, :, t * 128:(t + 1) * 128], pk[:, 0:256].re("p (j t) -> p j t", t=128))
                    if s == 3:
                        K.store(qT_v[:, :, blk * 512:(blk + 1) * 512], qst)

                skewed(NT, [qA, qB])
                K.barrier()
                K.recycle(n_glob)

            with ExitStack() as p3:
                wao = load_w_bf16(K, p3, "wao", wao_d.rearrange("(j p) n -> p j n", p=128), 8, D)
                qbd = K.sb(p3, "qbd", [128, 8, 2, 512], BF16, nbuf=2)
                for qq in qbd:
                    K.memset(qq, 0.0, eng="pool")
                PT = K.sb(p3, "PT", [128, 1024], BF16, nbuf=4)
                Osb = K.sb(p3, "Osb", [128, 512], F32, nbuf=2)
                rd = K.sb(p3, "rd", [128, 512], F32, nbuf=2)
                attnT = K.sb(p3, "attnT", [128, 8, 512], BF16, nbuf=2)
                aot = K.sb(p3, "aot", [128, D], F32, nbuf=2)
                qT_v = qT_s.rearrange("j p t -> p j t")
                NQB = T // 512

                def load_q(qb):
                    qq = qbd[qb % 2]
                    srcv = qT_v[:, :, qb * 512:(qb + 1) * 512].rearrange("p j (m t) -> p j m t", m=2)
                    for m in range(2):
                        K.load(qq[0:64, :, m, 0:256], srcv[0:64, :, m, :])
                        K.load(qq[64:128, :, m, 256:512], srcv[64:128, :, m, :])

                load_q(0)
                for qb in range(NQB):
                    if qb + 1 < NQB:
                        load_q(qb + 1)
                    ql = qbd[qb % 2]
                    aT = attnT[qb % 2]
                    NUq = 8 * NT
                    SKEW = 2
                    for n in range(NUq + SKEW):
                        if n < NUq:
                            p, kt = divmod(n, NT)
                            slot = p // 4
                            gi = qb * NUq + n
                            Sb = K.ps(2 + 2 * (gi % 3), 2)
                            for m in range(2):
                                K.mm(Sb[:, m * 512:(m + 1) * 512], KT[:, slot, kt * 128:(kt + 1) * 128],
                                     ql[:, p, m, :], inc=(m == 1))
                            K.actf(PT[gi % 4], Sb, AF.Exp, scale=DH ** -0.5)
                        if n >= SKEW:
                            n2 = n - SKEW
                            p, kt = divmod(n2, NT)
                            slot = p // 4
                            gi2 = qb * NUq + n2
                            P4 = PT[gi2 % 4].re("p (m h t) -> p m h t", m=2, h=2)
                            for half in range(2):
                                g = slot * 2 + half
                                K.mm(K.ps(half), VA[:, kt, g, :], P4[:, :, half, :],
                                     start=(kt == 0), stop=(kt == NT - 1))
                            if kt == NT - 1:
                                for half in range(2):
                                    K.cp(Osb[half], K.ps(half), eng="act")
                                    if half == 0:
                                        K.recip(rd[half][0:64, :], Osb[half][64:128, :])
                                        K.tt(aT[0:64, p, :], Osb[half][0:64, :], rd[half][0:64, :], ALU.mult)
                                    else:
                                        K.recip(rd[half][64:128, :], Osb[half][0:64, :])
                                        K.tt(aT[64:128, p, :], Osb[half][64:128, :], rd[half][64:128, :], ALU.mult)
                    for s in range(4):
                        for n in range(2):
                            for p in range(8):
                                K.mm(K.ps(6 + n), aT[:, p, s * 128:(s + 1) * 128], wao[:, p, n * 512:(n + 1) * 512],
                                     start=(p == 0), stop=(p == 7), inc=(p == 7))
                        ao = aot[s % 2]
                        K.cp(ao, K.ps(6, 2))
                        K.store(ao_s[qb * 512 + s * 128:qb * 512 + (s + 1) * 128, :], ao)
                K.barrier()
                K.recycle(n_glob)

        if "ssd" in phases:
          with ExitStack() as pss:
            wxbc = load_w_bf16(K, pss, "wxbc", wxbc_d.rearrange("(j p) n -> p j n", p=128), 8, CONV)
            wdt = load_w_bf16(K, pss, "wdt", wdt_d.rearrange("(j p) n -> p j n", p=128), 8, 64)
            msk = K.sb(pss, "msk", [128, 5, 128], F32)
            K.load(msk, masks_d)
            convw = K.sb(pss, "convw", [128, 24, 5], F32)
            K.load(convw, convw_d)
            convb = K.sb(pss, "convb", [128, 24], F32)
            K.load(convb, convb_d)
            dtb = K.sb(pss, "dtb", [128, 64], F32)
            K.load(dtb, dtb_d.to_broadcast([128, 64]))
            Ab = K.sb(pss, "Ab", [128, 64], F32)
            K.load(Ab, alog_d.to_broadcast([128, 64]))
            K.actf(Ab, Ab, AF.Exp)
            K.ts(Ab, Ab, -1.0, ALU.mult)
            Db = K.sb(pss, "Db", [128, DI], F32)
            K.load(Db, dfull_d.to_broadcast([128, DI]))
            diag = K.sb(pss, "diag", [128, 24, 5, 128], BF16)
            for ct in range(24):
                for k in range(5):
                    K.ts(diag[:, ct, k, :], identf, convw[:, ct, k:k + 1], ALU.mult)
            hw = K.sb(pss, "hw", [128, 8, 512 + 2 * PAD], BF16, nbuf=2)
            xbcT = K.sb(pss, "xbcT", [128, 24, 128 + 2 * PAD], BF16)
            xcTb = K.sb(pss, "xcT", [128, 24, 128], BF16, nbuf=2)
            dtrb = K.sb(pss, "dtrb", [128, 64], F32, nbuf=2)
            Btm = K.sb(pss, "Btm", [128, 512], BF16)
            st = K.sb(pss, "st", [128, NH, 64], F32)
            stbf = K.sb(pss, "stbf", [128, DI], BF16)
            rhi = K.sb(pss, "rhi", [128, NH, 128], BF16)
            rlo = K.sb(pss, "rlo", [128, NH, 128], BF16)
            mskb = K.sb(pss, "mskb", [128, 5, 128], BF16)
            K.cp(mskb, msk)
            Ebh = K.sb(pss, "Eb", [128, 16, 128], BF16, nbuf=2)
            CBm = K.sb(pss, "CBm", [128, 4, 128], BF16)
            Xb = K.sb(pss, "Xb", [128, DI], BF16)
            Xd = K.sb(pss, "Xd", [128, DI], BF16)
            ytmp = K.sb(pss, "ytmp", [128, D], F32)
            yacc = K.sb(pss, "yacc", [128, D], F32, nbuf=1) * 2
            yfin = K.sb(pss, "yfin", [128, DI], F32, nbuf=2)
            smb = [{n: K.sb(pss, "sm%d_%s" % (i, n), [128, 32], F32) for n in
                    ("x", "ab", "e", "l", "dt", "a", "acs", "expA", "dd", "dec", "cd", "w2")} for i in range(2)]
            smh = [(K.sb(pss, "ahi%d" % i, [128, 32], BF16), K.sb(pss, "alo%d" % i, [128, 32], BF16)) for i in range(2)]

            for d in range(SSD_PASSES):
                Ud = msk[:, d, :]
                Lsd = msk[:, 2 + d, :]
                ones = msk[:, 4, :]
                K.memset(st, 0.0)
                K.memset(stbf, 0.0, eng="pool")
                order = list(range(NT))[:SSD_NCH]
                if d == 1:
                    order = order[::-1]
                for ci, c in enumerate(order):
                    grp = c // 4
                    recompute = (d == 0) or not CACHE_CONV
                    if ci % 4 == 0 and recompute:
                        K.load(hw[(ci // 4) % 2], hT_v[:, :, grp * 512:grp * 512 + 512 + 2 * PAD])
                    hwv = hw[(ci // 4) % 2]
                    o = (c % 4) * 128
                    yf = yfin[ci % 2]
                    if d == 1:
                        K.load(yf, y_s[c * 128:(c + 1) * 128, :])
                    xcT = xcTb[ci % 2]
                    sm = smb[ci % 2]
                    p7 = K.ps(7)
                    dtr = dtrb[ci % 2]
                    if recompute:
                        for j in range(8):
                            K.mm(p7[:, 0:64], hwv[:, j, o + PAD:o + PAD + 128], wdt[:, j, :], start=(j == 0), stop=(j == 7), inc=(j == 7))
                        K.cp(dtr, p7[:, 0:64])
                        if CACHE_CONV:
                            K.store(dtr_view(c), dtr)
                    else:
                        if ci == 0:
                            K.load(xcTb[0].re("p a t -> p (a t)"), xc_view(order[0]))
                            K.load(dtrb[0], dtr_view(order[0]))
                        if ci + 1 < len(order):
                            K.load(xcTb[(ci + 1) % 2].re("p a t -> p (a t)"), xc_view(order[ci + 1]))
                            K.load(dtrb[(ci + 1) % 2], dtr_view(order[ci + 1]))
                    K.tt(sm["x"], dtr[:, d * 32:(d + 1) * 32], dtb[:, d * 32:(d + 1) * 32], ALU.add)
                    K.stt(sm["ab"], sm["x"], -1.0, sm["x"], ALU.mult, ALU.max)
                    K.actf(sm["e"], sm["ab"], AF.Exp, scale=-1.0)
                    K.actf(sm["l"], sm["e"], AF.Ln, bias=1.0)
                    K.stt(sm["dt"], sm["x"], 0.0, sm["l"], ALU.max, ALU.add)
                    K.tt(sm["a"], sm["dt"], Ab[:, d * 32:(d + 1) * 32], ALU.mult)
                    for w in (range(8) if recompute else ()):
                        bank = 4 + w % 3
                        for i in range(3):
                            ct = 3 * w + i
                            for j in range(8):
                                K.mm(K.ps(bank)[:, i * 132:(i + 1) * 132], wxbc[:, j, ct * 128:(ct + 1) * 128],
                                     hwv[:, j, o:o + 132], start=(j == 0), stop=(j == 7), inc=(j == 7))
                        K.cp(xbcT[:, 3 * w:3 * w + 3, :].re("p a t -> p (a t)"), K.ps(bank)[:, 0:396], eng="act")
                    K.mm(p7[:, 64:96], Ud, sm["a"])
                    K.mm(p7[:, 96:128], ones, sm["a"])
                    K.cp(sm["acs"], p7[:, 64:96])
                    K.actf(sm["expA"], p7[:, 64:96], AF.Exp)
                    K.tt(sm["dd"], p7[:, 96:128], sm["acs"], ALU.subtract)
                    K.actf(sm["dec"], sm["dd"], AF.Exp)
                    K.actf(sm["cd"], p7[:, 96:128], AF.Exp)
                    K.tt(sm["w2"], sm["dt"], sm["dec"], ALU.mult)
                    ahi, alo = smh[ci % 2]
                    K.cp(ahi, sm["a"])
                    K.tt(alo, sm["a"], ahi, ALU.subtract)
                    Ub3 = mskb[:, d, :].re("p (g l) -> p g l", g=1).bc([128, NH, 128])
                    K.tt(rhi, Ub3, ahi.re("p (h o) -> p h o", o=1).bc([128, NH, 128]), ALU.mult)
                    K.tt(rlo, Ub3, alo.re("p (h o) -> p h o", o=1).bc([128, NH, 128]), ALU.mult, eng="pool")
                    for ct in (range(24) if recompute else ()):
                        po = K.ps(ct % 4)[:, ((ct // 4) % 4) * 128:((ct // 4) % 4 + 1) * 128]
                        for k in range(5):
                            K.mm(po, diag[:, ct, k, :], xbcT[:, ct, k:k + 128], start=(k == 0), stop=(k == 4), inc=(k == 4))
                        K.actf(xcT[:, ct, :], po, AF.Silu, bias=convb[:, ct:ct + 1])
                    if recompute and CACHE_CONV:
                        K.store(xc_view(c), xcT.re("p a t -> p (a t)"))
                    xs_ps = K.ps(4, 2, dt=BF16)
                    for ct in range(16):
                        K.tr(xs_ps[:, ct * 128:(ct + 1) * 128], xcT[:, ct, :], identb)
                    b_ps = K.ps(6, dt=BF16)
                    for g in range(4):
                        K.tr(b_ps[:, g * 128:(g + 1) * 128], xcT[:, 16 + g, :], identb)
                    K.cp(Btm, b_ps[:, 0:512])
                    xs3 = xs_ps.re("p (h e) -> p h e", e=64)
                    K.tt(Xb.re("p (h e) -> p h e", e=64), xs3, sm["dt"].re("p (h o) -> p h o", o=1).bc([128, NH, 64]), ALU.mult)
                    K.tt(Xd.re("p (h e) -> p h e", e=64), xs3, sm["w2"].re("p (h o) -> p h o", o=1).bc([128, NH, 64]), ALU.mult)
                    if d == 0:
                        K.tt(yf, xs_ps, Db, ALU.mult)
                    p3 = K.ps(3)
                    for g in range(4):
                        K.mm(p3[:, g * 128:(g + 1) * 128], xcT[:, 16 + g, :], xcT[:, 20 + g, :])
                    K.tt(CBm, p3.re("p (g l) -> p g l", l=128), Ud.re("p (g l) -> p g l", g=1).bc([128, 4, 128]), ALU.mult)
                    Lsb = mskb[:, 2 + d, :]
                    for hq in range(8):
                        pd = K.ps(hq % 2)
                        K.mm(pd, Lsb, rhi[:, 4 * hq:4 * hq + 4, :].re("p h l -> p (h l)"), start=True, stop=False, inc=False)
                        K.mm(pd, Lsb, rlo[:, 4 * hq:4 * hq + 4, :].re("p h l -> p (h l)"), start=False, stop=True)
                        K.actf(Ebh[hq // 4][:, 4 * (hq % 4):4 * (hq % 4) + 4, :].re("p h l -> p (h l)"), pd, AF.Exp)
                        if hq % 2 == 1:
                            g = hq // 2
                            mtv = Ebh[g // 2][:, 8 * (g % 2):8 * (g % 2) + 8, :]
                            K.tt(mtv, mtv, CBm[:, g:g + 1, :].bc([128, 8, 128]), ALU.mult)
                    for hh in range(2):
                        ydg = K.ps(2, 2)
                        for h in range(16 * hh, 16 * hh + 16):
                            K.mm(ydg[:, (h % 16) * 64:(h % 16 + 1) * 64], Ebh[hh][:, h % 16, :], Xb[:, h * 64:(h + 1) * 64],
                                 inc=(h % 16 == 15))
                        yof = K.ps(6, 2)
                        for g in (2 * hh, 2 * hh + 1):
                            K.mm(yof[:, (g % 2) * 512:(g % 2 + 1) * 512], xcT[:, 20 + g, :], stbf[:, g * 512:(g + 1) * 512],
                                 inc=(g % 2 == 1))
                        K.tt(ytmp.re("p (h e) -> p h e", e=64), yof.re("p (h e) -> p h e", e=64),
                             sm["expA"][:, 16 * hh:16 * hh + 16].re("p (h o) -> p h o", o=1).bc([128, 16, 64]), ALU.mult)
                        ya = yacc[hh]
                        K.tt(ya, ytmp, ydg, ALU.add)
                        K.tt(yf[:, hh * D:(hh + 1) * D], ya, yf[:, hh * D:(hh + 1) * D], ALU.add, eng="pool")
                    K.store(y_s[c * 128:(c + 1) * 128, :], yf)
                    pst = K.ps(0, 4)
                    for g in range(4):
                        K.mm(pst[:, g * 512:(g + 1) * 512], Btm[:, g * 128:(g + 1) * 128], Xd[:, g * 512:(g + 1) * 512])
                    K.tt(st, st, sm["cd"].re("p (h o) -> p h o", o=1).bc([128, NH, 64]), ALU.mult, eng="pool")
                    K.tt(st.re("p h e -> p (h e)"), st.re("p h e -> p (h e)"), pst, ALU.add)
                    K.cp(stbf, st.re("p h e -> p (h e)"), eng="act")
                K.barrier()
            K.recycle(n_glob)

          with ExitStack() as pc:
            wz = load_w_bf16(K, pc, "wz", wz_d.rearrange("(j p) n -> p j n", p=128), 8, DI)
            wso = load_w_bf16(K, pc, "wso", wso_d.rearrange("(j p) n -> p j n", p=128), 16, D)
            snw = K.sb(pc, "snw", [128, DI], F32)
            K.load(snw, snw_d.to_broadcast([128, DI]))
            hTl = K.sb(pc, "hTl4", [128, 8, 512], BF16, nbuf=2)
            yl = K.sb(pc, "yl", [128, DI], F32, nbuf=3)
            zs = K.sb(pc, "zs", [128, DI], F32)
            yz = K.sb(pc, "yz", [128, DI], F32)
            junk2 = K.sb(pc, "junk2", [128, DI], BF16)
            yn = K.sb(pc, "yn", [128, DI], BF16, nbuf=2)
            ssdT = K.sb(pc, "ssdT", [128, 16, 128], BF16, nbuf=2)
            sot = K.sb(pc, "sot", [128, D], F32, nbuf=2)
            s1 = K.sb(pc, "s1", [128, 1], F32, nbuf=2)
            s2 = K.sb(pc, "s2", [128, 1], F32, nbuf=2)
            s3 = K.sb(pc, "s3", [128, 1], F32, nbuf=2)
            def cA(t):
                blk, s = t // 4, t % 4
                if s == 0:
                    K.load(hTl[blk % 2], hT_v[:, :, PAD + blk * 512:PAD + (blk + 1) * 512])
                hl = hTl[blk % 2]
                K.load(yl[t % 3], y_s[t * 128:(t + 1) * 128, :])
                for n in range(4):
                    for j in range(8):
                        K.mm(K.ps(n), hl[:, j, s * 128:(s + 1) * 128], wz[:, j, n * 512:(n + 1) * 512],
                             start=(j == 0), stop=(j == 7), inc=(j == 7))
                K.actf(zs, K.ps(0, 4), AF.Silu)
                K.tt(yz, yl[t % 3], zs, ALU.mult)
                K.actf(junk2, yz, AF.Square, accum=s1[t % 2])
                K.actf(s2[t % 2], s1[t % 2], AF.Sqrt, bias=EPS, scale=1.0 / DI)
                K.recip(s3[t % 2], s2[t % 2])
                K.stt(yn[t % 2], yz, s3[t % 2], snw, ALU.mult, ALU.mult)

            def cB(t):
                pT = K.ps(4, 2, dt=BF16)
                for j in range(16):
                    K.tr(pT[:, j * 128:(j + 1) * 128], yn[t % 2][:, j * 128:(j + 1) * 128], identb)
                K.cp(ssdT[t % 2].re("p j t -> p (j t)"), pT)

            def cC(t):
                for n in range(2):
                    for j in range(16):
                        K.mm(K.ps(6 + n), ssdT[t % 2][:, j, :], wso[:, j, n * 512:(n + 1) * 512],
                             start=(j == 0), stop=(j == 15), inc=(j == 15))
                K.cp(sot[t % 2], K.ps(6, 2), eng="act")
                K.store(so_s[t * 128:(t + 1) * 128, :], sot[t % 2])

            skewed(min(NT, SSD_NCH), [cA, cB, cC])
            K.barrier()
            K.recycle(n_glob)

        if "merge" in phases:
          with ExitStack() as pm:
            wg = load_w_bf16(K, pm, "wg", wg_d.rearrange("(j p) n -> p j n", p=128), 8, 2 * D)
            wo = load_w_bf16(K, pm, "wo", wo_d.rearrange("(j p) n -> p j n", p=128), 8, D)
            g2 = K.sb(pm, "g2m", [128, D], F32)
            sh2 = K.sb(pm, "sh2", [128, D], F32)
            gt1 = K.sb(pm, "gt1", [128, D], F32)
            K.load(g2, mod_s[:, 0:D])
            K.load(sh2, mod_s[:, D:2 * D])
            K.load(gt1, mod_s[:, 2 * D:3 * D])
            hTl = K.sb(pm, "hTl5", [128, 8, 512], BF16, nbuf=2)
            aol = K.sb(pm, "aol", [128, D], F32, nbuf=2)
            sol = K.sb(pm, "sol", [128, D], F32, nbuf=2)
            xl = K.sb(pm, "xl", [128, D], F32, nbuf=3)
            sig = K.sb(pm, "sig", [128, 2 * D], F32)
            m1 = K.sb(pm, "m1", [128, D], F32)
            m2 = K.sb(pm, "m2", [128, D], F32)
            mg = K.sb(pm, "mg", [128, D], BF16, nbuf=2)
            mT = K.sb(pm, "mT", [128, 8, 128], BF16, nbuf=2)
            tq = K.sb(pm, "tq", [128, D], F32)
            x1 = K.sb(pm, "x1", [128, D], F32, nbuf=2)
            junk3 = K.sb(pm, "junk3", [128, D], BF16)
            u1 = K.sb(pm, "u1", [128, D], F32)
            u2 = K.sb(pm, "u2", [128, D], F32)
            h2 = K.sb(pm, "h2", [128, D], BF16, nbuf=2)
            h2Ts = K.sb(pm, "h2Ts", [128, 8, 512], BF16, nbuf=2)
            a1 = K.sb(pm, "a1", [128, 1], F32, nbuf=2)
            a2 = K.sb(pm, "a2", [128, 1], F32, nbuf=2)
            a3 = K.sb(pm, "a3", [128, 1], F32, nbuf=2)
            h2T_v = h2T_s.rearrange("j p t -> p j t")
            def mA(t):
                blk, s = t // 4, t % 4
                if s == 0:
                    K.load(hTl[blk % 2], hT_v[:, :, PAD + blk * 512:PAD + (blk + 1) * 512])
                hl = hTl[blk % 2]
                rows = slice(t * 128, (t + 1) * 128)
                K.load(aol[t % 2], ao_s[rows, :])
                K.load(sol[t % 2], so_s[rows, :])
                K.load(xl[t % 3], x_d[rows, :])
                for n in range(4):
                    for j in range(8):
                        K.mm(K.ps(n), hl[:, j, s * 128:(s + 1) * 128], wg[:, j, n * 512:(n + 1) * 512],
                             start=(j == 0), stop=(j == 7), inc=(j == 7))
                K.actf(sig, K.ps(0, 4), AF.Sigmoid)
                K.tt(m1, sig[:, 0:D], aol[t % 2], ALU.mult)
                K.tt(m2, sig[:, D:2 * D], sol[t % 2], ALU.mult, eng="pool")
                K.tt(mg[t % 2], m1, m2, ALU.add)

            def mB(t):
                pT = K.ps(4, dt=BF16)
                for j in range(8):
                    K.tr(pT[:, j * 128:(j + 1) * 128], mg[t % 2][:, j * 128:(j + 1) * 128], identb)
                K.cp(mT[t % 2].re("p j t -> p (j t)"), pT, eng="act")

            def mC(t):
                rows = slice(t * 128, (t + 1) * 128)
                for n in range(2):
                    for j in range(8):
                        K.mm(K.ps(5 + n), mT[t % 2][:, j, :], wo[:, j, n * 512:(n + 1) * 512],
                             start=(j == 0), stop=(j == 7), inc=(j == 7))
                K.tt(tq, K.ps(5, 2), gt1, ALU.mult)
                xx = x1[t % 2]
                K.tt(xx, tq, xl[t % 3], ALU.add, eng="pool")
                K.store(x1_s[rows, :], xx)
                K.actf(junk3, xx, AF.Square, accum=a1[t % 2])
                K.actf(a2[t % 2], a1[t % 2], AF.Sqrt, bias=EPS, scale=1.0 / D)
                K.recip(a3[t % 2], a2[t % 2])
                K.actf(u1, xx, AF.Copy, scale=a3[t % 2])
                K.tt(u2, u1, g2, ALU.mult, eng="pool")
                K.tt(h2[t % 2], u2, sh2, ALU.add)

            def mD(t):
                blk, s = t // 4, t % 4
                pT2 = K.ps(7, dt=BF16)
                for j in range(8):
                    K.tr(pT2[:, j * 128:(j + 1) * 128], h2[t % 2][:, j * 128:(j + 1) * 128], identb)
                K.cp(h2Ts[blk % 2][:, :, s * 128:(s + 1) * 128], pT2.re("p (j t) -> p j t", t=128))
                if s == 3:
                    K.store(h2T_v[:, :, blk * 512:(blk + 1) * 512], h2Ts[blk % 2])

            skewed(NT, [mA, mB, mC, mD])
            K.barrier()
            K.recycle(n_glob)

        if "mlp" in phases:
          with ExitStack() as pf:
            w1 = load_w_bf16(K, pf, "w1", w1_d.rearrange("(j p) n -> p j n", p=128), 8, DFF)
            w2 = load_w_bf16(K, pf, "w2", w2_d.rearrange("(j p) n -> p j n", p=128), 32, D)
            gt2 = K.sb(pf, "gt2", [128, D], F32)
            K.load(gt2, mod_s[:, 3 * D:4 * D])
            TB = 256
            h2l = K.sb(pf, "h2l", [128, 8, TB], BF16, nbuf=2)
            uT = K.sb(pf, "uT", [128, 32, TB], BF16)
            rl = K.sb(pf, "rl", [128, TB], F32, nbuf=2)
            x1l = K.sb(pf, "x1l", [128, D], F32, nbuf=2)
            tf = K.sb(pf, "tf", [128, D], F32)
            ot = K.sb(pf, "ot", [128, D], F32, nbuf=2)
            h2T_v = h2T_s.rearrange("j p t -> p j t")
            NB = T // TB
            K.load(h2l[0], h2T_v[:, :, 0:TB])
            for blk in range(NB):
                if blk + 1 < NB:
                    K.load(h2l[(blk + 1) % 2], h2T_v[:, :, (blk + 1) * TB:(blk + 2) * TB])
                hh = h2l[blk % 2]
                for fc in range(32):
                    pf_ = K.ps(fc % 2)[:, 0:TB]
                    for j in range(8):
                        K.mm(pf_, w1[:, j, fc * 128:(fc + 1) * 128], hh[:, j, :], start=(j == 0), stop=(j == 7), inc=(j == 7))
                    K.actf(rl[fc % 2], pf_, AF.Relu)
                    K.tt(uT[:, fc, :], rl[fc % 2], rl[fc % 2], ALU.mult, eng=("dve" if fc % 2 == 0 else "pool"))
                for s in range(TB // 128):
                    tt_ = blk * (TB // 128) + s
                    rows = slice(tt_ * 128, (tt_ + 1) * 128)
                    K.load(x1l[tt_ % 2], x1_s[rows, :])
                    pb = 2 + 2 * (tt_ % 2)
                    for n in range(2):
                        for fc in range(32):
                            K.mm(K.ps(pb + n), uT[:, fc, s * 128:(s + 1) * 128], w2[:, fc, n * 512:(n + 1) * 512],
                                 start=(fc == 0), stop=(fc == 31), inc=(fc == 31))
                    K.tt(tf, K.ps(pb, 2), gt2, ALU.mult)
                    K.tt(ot[tt_ % 2], tf, x1l[tt_ % 2], ALU.add, eng="pool")
                    K.store(out_d[rows, :], ot[tt_ % 2])
            K.barrier()

        K.barrier()
    return K


def _bf16(a):
    return np.asarray(a, dtype=np.float32).astype(ml_dtypes.bfloat16)


def make_inputs(inputs, b):
    m = {}
    m["x"] = np.ascontiguousarray(inputs["x"][b])
    m["c"] = np.ascontiguousarray(np.asarray(inputs["c"][b]).reshape(8, 128).T)
    m["w_ada"] = np.ascontiguousarray(inputs["w_ada"][0])
    m["b_ada"] = np.ascontiguousarray(inputs["b_ada"][0].reshape(1, -1))
    m["norm1_w"] = np.ascontiguousarray(inputs["norm1_w"][0].reshape(1, -1))
    m["norm2_w"] = np.ascontiguousarray(inputs["norm2_w"][0].reshape(1, -1))
    m["ident"] = np.eye(128, dtype=np.float32)
    w_in = inputs["w_in"][0]
    wq = w_in[:, 0:1024].reshape(D, 16, 64)[:, QPERM, :].reshape(D, 1024)
    m["w_qkv"] = np.ascontiguousarray(np.concatenate([wq, w_in[:, 1024:1536]], axis=1))
    m["wqk"] = np.ascontiguousarray(np.concatenate([np.tile(inputs["q_norm_w"][0], 16),
                                                      np.tile(inputs["k_norm_w"][0], 4)]).reshape(1, 1280))
    m["cs"] = rope_table()
    m["w_ao"] = np.ascontiguousarray(inputs["w_attn_out"][0].reshape(16, 64, D)[QPERM].reshape(D, D))
    m["w_xbc"] = np.ascontiguousarray(w_in[:, 1536:4608])
    m["w_z"] = np.ascontiguousarray(w_in[:, 4608:6656])
    m["w_dt"] = np.ascontiguousarray(w_in[:, 6656:6720])
    m["w_g"] = np.ascontiguousarray(w_in[:, 6720:8768])
    m["convw"] = np.ascontiguousarray(inputs["conv_w"][0].reshape(5, 24, 128).transpose(2, 1, 0))
    m["convb"] = np.ascontiguousarray(inputs["conv_b"][0].reshape(24, 128).T)
    m["dtb"] = np.ascontiguousarray(inputs["dt_bias"][0].reshape(1, 64))
    m["alog"] = np.ascontiguousarray(inputs["A_log"][0].reshape(1, 64))
    m["dfull"] = np.ascontiguousarray(np.repeat(inputs["ssd_D"][0], 64).reshape(1, DI))
    m["snw"] = np.ascontiguousarray(inputs["ssd_norm_w"][0].reshape(1, DI))
    m["w_so"] = np.ascontiguousarray(inputs["w_ssd_out"][0])
    i = np.arange(128)
    uf = (i[:, None] <= i[None, :]).astype(np.float32)
    lf = (i[:, None] > i[None, :]).astype(np.float32)
    m["masks"] = np.ascontiguousarray(np.stack([uf, uf.T, lf, lf.T, np.ones((128, 128), np.float32)], axis=1))
    m["w_o"] = np.ascontiguousarray(inputs["w_o"][0])
    m["w_1"] = np.ascontiguousarray(inputs["w_mlp1"][0])
    m["w_2"] = np.ascontiguousarray(inputs["w_mlp2"][0])
    return m


def rope_table():
    pos = np.arange(T)
    pr = (pos // 64).astype(np.float32)
    pc = (pos % 64).astype(np.float32)
    inv = (np.float32(10000.0) ** (-np.arange(0, 32, 2, dtype=np.float32) / np.float32(32))).astype(np.float32)
    ar = pr[:, None] * inv[None, :]
    ac = pc[:, None] * inv[None, :]
    return np.ascontiguousarray(np.concatenate([np.cos(ar), np.cos(ac), np.sin(ar), np.sin(ac)], axis=1).astype(np.float32))


def kernel(**inputs):
    inputs = {k: np.asarray(v) for k, v in inputs.items()}
    K = build()
    in_maps = [make_inputs(inputs, b) for b in range(8)]
    res = run_bass_kernel_spmd(K.nc, in_maps, core_ids=list(range(8)))
    return np.stack([np.asarray(r["out"]) for r in res.results], axis=0).astype(np.float32)
```
